# Optimizing a Trainium2 kernel written in Bass

```python
import math
import jax, jax.numpy as jnp
from jax import lax
import numpy as np

D_MODEL = 1024
BATCH = 16
SEQ = 4096
DEPTH = 2

N_META = 16
CHUNK = 64
N_PAD = CHUNK - N_META
GDN_HEADS = D_MODEL // 128
GDN_DK = 128
GDN_DV = 128
GDN_QK = GDN_HEADS * GDN_DK
GDN_V = GDN_HEADS * GDN_DV
CONV_K = 5
CONV_DIM = 2 * GDN_QK + GDN_V
RET_HEADS = D_MODEL // 256
RET_DK = 256
RET_DV = 512
RET_QK = RET_HEADS * RET_DK
RET_V = RET_HEADS * RET_DV
ROPE_BASE = 10000.0
FFN_HIDDEN = -(-8 * D_MODEL // (3 * 256)) * 256
SPLIT_SIZES = (CONV_DIM, GDN_V, 2 * GDN_HEADS, 2 * GDN_HEADS, RET_QK, RET_QK, RET_V, RET_V, D_MODEL, D_MODEL)
N_IN = CONV_DIM + GDN_V + 4 * GDN_HEADS + 2 * RET_QK + 2 * RET_V + 2 * D_MODEL
EPS = 1e-6

kernel_name = "hybrid_gdn_retention_encoder"


def rms(x):
    xf = x.astype(jnp.float32)
    return xf * lax.rsqrt(jnp.mean(xf * xf, axis=-1, keepdims=True) + EPS)


def rms_norm(x, gain):
    return (rms(x) * gain.astype(jnp.float32)).astype(x.dtype)


def l2norm(x):
    return x * lax.rsqrt(jnp.sum(x * x, axis=-1, keepdims=True) + EPS)


def split_points(sizes):
    return [int(v) for v in np.cumsum(np.array(sizes))[:-1]]


def short_conv(x, w):
    c = x.shape[-1]
    pad = (w.shape[0] - 1) // 2
    y = lax.conv_general_dilated(x, w[:, None, :].astype(x.dtype), window_strides=(1,),
                                 padding=[(pad, pad)], dimension_numbers=('NWC', 'WIO', 'NWC'),
                                 feature_group_count=c)
    return jax.nn.silu(y)


def rope(t, pos):
    half = t.shape[-1] // 2
    inv = ROPE_BASE ** (-jnp.arange(half, dtype=jnp.float32) / half)
    ang = pos.astype(jnp.float32)[:, None] * inv[None, :]
    cos = jnp.cos(ang)[None, :, None, :]
    sin = jnp.sin(ang)[None, :, None, :]
    t1, t2 = t[..., :half], t[..., half:]
    return jnp.concatenate([t1 * cos - t2 * sin, t1 * sin + t2 * cos], axis=-1)


def pad_front(t):
    pads = [(0, 0)] * t.ndim
    pads[1] = (N_PAD, 0)
    return jnp.pad(t, pads)


def flip(t):
    return jnp.flip(t, axis=1)


def to_chunks(t):
    b, lp, h, d = t.shape
    return t.reshape(b, lp // CHUNK, CHUNK, h, d).transpose(1, 0, 3, 2, 4)


def from_chunks(t):
    n, b, h, c, d = t.shape
    return t.transpose(1, 0, 3, 2, 4).reshape(b, n * c, h, d)


def gated_delta_chunked(q, k, v, g, beta):
    dk = q.shape[-1]
    b, _, h, dv = v.shape
    q = to_chunks(q * (dk ** -0.5))
    k = to_chunks(k)
    v = to_chunks(v)
    g = to_chunks(g[..., None])[..., 0]
    beta = to_chunks(beta[..., None])[..., 0]
    gc = jnp.cumsum(g, axis=-1)
    idx = jnp.arange(CHUNK)
    lower_incl = idx[:, None] >= idx[None, :]
    lower_strict = idx[:, None] > idx[None, :]
    diff = gc[..., :, None] - gc[..., None, :]
    decay = jnp.where(lower_incl, jnp.exp(jnp.where(lower_incl, diff, 0.0)), 0.0)
    kb = k * beta[..., None]
    m = jnp.einsum('nbhid,nbhjd->nbhij', kb, k) * decay * lower_strict
    a = m + jnp.eye(CHUNK, dtype=m.dtype)
    u = lax.linalg.triangular_solve(a, v * beta[..., None], left_side=True, lower=True, unit_diagonal=True)
    w = lax.linalg.triangular_solve(a, kb * jnp.exp(gc)[..., None], left_side=True, lower=True, unit_diagonal=True)
    qk = jnp.einsum('nbhid,nbhjd->nbhij', q, k) * decay
    q_dec = q * jnp.exp(gc)[..., None]
    g_last = gc[..., -1]
    k_dec = k * jnp.exp(g_last[..., None] - gc)[..., None]

    def step(s, xs):
        u_i, w_i, qk_i, qd_i, kd_i, gl_i = xs
        v_new = u_i - jnp.einsum('bhcd,bhde->bhce', w_i, s)
        o = jnp.einsum('bhcd,bhde->bhce', qd_i, s) + jnp.einsum('bhij,bhje->bhie', qk_i, v_new)
        s = s * jnp.exp(gl_i)[..., None, None] + jnp.einsum('bhcd,bhce->bhde', kd_i, v_new)
        return s, o

    s0 = jnp.zeros((b, h, dk, dv), jnp.float32)
    _, o = lax.scan(step, s0, (u, w, qk, q_dec, k_dec, g_last))
    return from_chunks(o)


def retention_chunked(q, k, v, log_gamma):
    dk = q.shape[-1]
    b, _, h, dv = v.shape
    q = to_chunks(q)
    k = to_chunks(k * (dk ** -0.5))
    v = to_chunks(v)
    pos = jnp.arange(CHUNK, dtype=jnp.float32)
    lg = log_gamma.astype(jnp.float32)[:, None]
    lower_incl = pos[:, None] >= pos[None, :]
    rel = jnp.where(lower_incl, pos[:, None] - pos[None, :], 0.0)
    intra = jnp.where(lower_incl, jnp.exp(rel[None] * lg[..., None]), 0.0)
    qk = jnp.einsum('nbhid,nbhjd->nbhij', q, k) * intra
    q_dec = q * jnp.exp(lg * (pos + 1.0))[..., None]
    k_dec = k * jnp.exp(lg * (CHUNK - 1.0 - pos))[..., None]
    chunk_decay = jnp.exp(lg * CHUNK)[..., None]

    def step(r, xs):
        qd_i, kd_i, qk_i, v_i = xs
        o = jnp.einsum('bhcd,bhde->bhce', qd_i, r) + jnp.einsum('bhij,bhje->bhie', qk_i, v_i)
        r = r * chunk_decay + jnp.einsum('bhcd,bhce->bhde', kd_i, v_i)
        return r, o

    r0 = jnp.zeros((b, h, dk, dv), jnp.float32)
    _, o = lax.scan(step, r0, (q_dec, k_dec, qk, v))
    return from_chunks(o)


def token_mixer(h, w_in, conv_w, a_log, dt_bias, gdn_gain, ret_logit, w_up_a, w_up_b, w_out):
    b, l, _ = h.shape
    f32 = jnp.float32
    proj = h @ w_in.astype(h.dtype)
    (qkv_a, z_a, a_in, b_in, q_b, k_b, v_b, g_b, gate_a, gate_b) = jnp.split(proj, split_points(SPLIT_SIZES), axis=-1)

    qkv = short_conv(qkv_a, conv_w).astype(f32)
    qa, ka, va = jnp.split(qkv, [GDN_QK, 2 * GDN_QK], axis=-1)
    qa = l2norm(qa.reshape(b, l, GDN_HEADS, GDN_DK))
    ka = l2norm(ka.reshape(b, l, GDN_HEADS, GDN_DK))
    va = va.reshape(b, l, GDN_HEADS, GDN_DV)
    a_in = a_in.astype(f32).reshape(b, l, 2, GDN_HEADS)
    g = -jnp.exp(a_log.astype(f32)) * jax.nn.softplus(a_in + dt_bias.astype(f32))
    beta = jax.nn.sigmoid(b_in.astype(f32).reshape(b, l, 2, GDN_HEADS))
    qp, kp, vp, gp, bp = pad_front(qa), pad_front(ka), pad_front(va), pad_front(g), pad_front(beta)
    o_fwd = gated_delta_chunked(qp, kp, vp, gp[:, :, 0], bp[:, :, 0])
    o_bwd = flip(gated_delta_chunked(flip(qp), flip(kp), flip(vp), flip(gp[:, :, 1]), flip(bp[:, :, 1])))
    o_a = (o_fwd + o_bwd)[:, N_PAD:]
    o_a = rms(o_a) * gdn_gain.astype(f32) * jax.nn.silu(z_a.astype(f32).reshape(b, l, GDN_HEADS, GDN_DV))
    y_a = o_a.reshape(b, l, GDN_V).astype(h.dtype) @ w_up_a.astype(h.dtype)

    pos = jnp.arange(l)
    qr = rope(q_b.astype(f32).reshape(b, l, RET_HEADS, RET_DK), pos)
    kr = rope(k_b.astype(f32).reshape(b, l, RET_HEADS, RET_DK), pos)
    vr = v_b.astype(f32).reshape(b, l, RET_HEADS, RET_DV)
    log_gamma = jax.nn.log_sigmoid(ret_logit.astype(f32))
    qp, kp, vp = pad_front(qr), pad_front(kr), pad_front(vr)
    r_fwd = retention_chunked(qp, kp, vp, log_gamma[0])
    r_bwd = flip(retention_chunked(flip(qp), flip(kp), flip(vp), log_gamma[1]))
    o_b = rms((r_fwd + r_bwd)[:, N_PAD:])
    o_b = o_b.reshape(b, l, RET_V) * jax.nn.silu(g_b.astype(f32))
    y_b = o_b.astype(h.dtype) @ w_up_b.astype(h.dtype)

    merged = jax.nn.sigmoid(gate_a) * y_a + jax.nn.sigmoid(gate_b) * y_b
    return merged @ w_out.astype(h.dtype)


def swiglu(h, w_ffn_in, w_ffn_out):
    gate, up = jnp.split(h @ w_ffn_in.astype(h.dtype), 2, axis=-1)
    return (jax.nn.silu(gate) * up) @ w_ffn_out.astype(h.dtype)


def setup_inputs(seed: int = 0) -> dict:
    key = jax.random.key(seed)
    ks = jax.random.split(key, 20)
    f32 = jnp.float32

    def dense(k, shape, fan_in):
        return jax.random.normal(k, shape, f32) * (fan_in ** -0.5)

    def gain(k, shape):
        return 1.0 + 0.02 * jax.random.normal(k, shape, f32)

    x = jax.random.normal(ks[0], (BATCH, SEQ, D_MODEL), f32)
    meta_tokens = jax.random.normal(ks[1], (N_META, D_MODEL), f32)
    norm_mix = gain(ks[2], (DEPTH, D_MODEL))
    w_in = dense(ks[3], (DEPTH, D_MODEL, N_IN), D_MODEL)
    conv_w = dense(ks[4], (DEPTH, CONV_K, CONV_DIM), CONV_K)
    gdn_a_log = jnp.log(jax.random.uniform(ks[5], (DEPTH, 2, GDN_HEADS), f32, 1.0, 16.0))
    dt = jnp.exp(jax.random.uniform(ks[6], (DEPTH, 2, GDN_HEADS), f32, math.log(1e-3), math.log(1e-1)))
    gdn_dt_bias = dt + jnp.log(-jnp.expm1(-dt))
    gdn_norm = gain(ks[7], (DEPTH, GDN_DV))
    base_logit = jnp.log(2.0 ** (5.0 + jnp.arange(RET_HEADS, dtype=f32)) - 1.0)
    ret_decay_logit = base_logit + 0.1 * jax.random.normal(ks[8], (DEPTH, 2, RET_HEADS), f32)
    w_up_a = dense(ks[9], (DEPTH, GDN_V, D_MODEL), GDN_V)
    w_up_b = dense(ks[10], (DEPTH, RET_V, D_MODEL), RET_V)
    w_out = dense(ks[11], (DEPTH, D_MODEL, D_MODEL), D_MODEL)
    norm_ffn = gain(ks[12], (DEPTH, D_MODEL))
    w_ffn_in = dense(ks[13], (DEPTH, D_MODEL, 2 * FFN_HIDDEN), D_MODEL)
    w_ffn_out = dense(ks[14], (DEPTH, FFN_HIDDEN, D_MODEL), FFN_HIDDEN)
    norm_final = gain(ks[15], (D_MODEL,))
    return {"x": x, "meta_tokens": meta_tokens, "norm_mix": norm_mix, "w_in": w_in, "conv_w": conv_w,
            "gdn_a_log": gdn_a_log, "gdn_dt_bias": gdn_dt_bias, "gdn_norm": gdn_norm,
            "ret_decay_logit": ret_decay_logit, "w_up_a": w_up_a, "w_up_b": w_up_b, "w_out": w_out,
            "norm_ffn": norm_ffn, "w_ffn_in": w_ffn_in, "w_ffn_out": w_ffn_out, "norm_final": norm_final}


def reference(x, meta_tokens, norm_mix, w_in, conv_w, gdn_a_log, gdn_dt_bias, gdn_norm, ret_decay_logit,
              w_up_a, w_up_b, w_out, norm_ffn, w_ffn_in, w_ffn_out, norm_final):
    b = x.shape[0]
    meta = jnp.broadcast_to(meta_tokens.astype(x.dtype)[None], (b, N_META, x.shape[-1]))
    h = jnp.concatenate([meta, x], axis=1)
    for i in range(DEPTH):
        h = h + token_mixer(rms_norm(h, norm_mix[i]), w_in[i], conv_w[i], gdn_a_log[i], gdn_dt_bias[i],
                            gdn_norm[i], ret_decay_logit[i], w_up_a[i], w_up_b[i], w_out[i])
        h = h + swiglu(rms_norm(h, norm_ffn[i]), w_ffn_in[i], w_ffn_out[i])
    h = rms_norm(h, norm_final)
    return h[:, N_META:]
```

```python
import math
from contextlib import ExitStack
import numpy as np
import ml_dtypes
import concourse.bass as bass
import concourse.mybir as mybir
from concourse.bass_utils import run_bass_kernel_spmd

F32 = mybir.dt.float32
BF16 = mybir.dt.bfloat16
AF = mybir.ActivationFunctionType
ALU = mybir.AluOpType
AX = mybir.AxisListType

D = 1024
N_META = 16
NIN = 12320
FFN = 2816
EPS = 1e-6
C = 64
CR = 128
NEG = -30000.0
INTERLEAVE_FB = False

O_QA, O_KA, O_VA, O_Z, O_A, O_B, O_QB, O_KB, O_VB, O_GB, O_GA, O_GBT = (
    0, 1024, 2048, 3072, 4096, 4112, 4128, 5152, 6176, 8224, 10272, 11296)


class Buf:
    __slots__ = ("name", "w", "r", "dsem")

    def __init__(self, name):
        self.name = name
        self.w = {}
        self.r = {}
        self.dsem = None


class Prog:
    ENG = ("pe", "act", "dve", "pool", "sp")

    def __init__(self, nc, stack, n_dsem=40):
        self.nc = nc
        self.stack = stack
        self.q = {e: [] for e in self.ENG}
        self.sems = {}
        self.cnt = {}
        self.known = {e: {} for e in self.ENG}
        for e in ("pe", "act", "dve", "pool"):
            self._newsem(e)
        self.dsems = []
        for i in range(n_dsem):
            nm = "d%d" % i
            self._newsem(nm)
            self.dsems.append(nm)
        self.dnext = 0
        self.nins = 0
        self.reuse_dsem = True

    def _newsem(self, name):
        self.sems[name] = self.stack.enter_context(self.nc.semaphore(name))
        self.cnt[name] = 0

    def _wait(self, E, deps):
        for s, c in deps.items():
            if E == "pe" and s == "pe":
                continue
            if self.known[E].get(s, 0) < c:
                self.q[E].append(("w", s, c))
                self.known[E][s] = c

    @staticmethod
    def _add(deps, d):
        for s, c in d.items():
            if deps.get(s, 0) < c:
                deps[s] = c

    def op(self, E, fn, reads=(), writes=(), accw=(), inc=True):
        deps = {}
        for b in reads:
            self._add(deps, b.w)
        for b in writes:
            self._add(deps, b.w)
            self._add(deps, b.r)
        for b in accw:
            self._add(deps, b.r)
            self._add(deps, b.w)
        self._wait(E, deps)
        s = E
        if inc:
            self.cnt[s] += 1
            c = self.cnt[s]
            self.q[E].append(("i", fn, s, 1))
        else:
            c = self.cnt[s] + 1
            self.q[E].append(("i", fn, None, 0))
        self.nins += 1
        for b in reads:
            if b.r.get(s, 0) < c:
                b.r[s] = c
        for b in writes:
            b.w = {s: c}
            b.r = {}
        for b in accw:
            if b.w.get(s, 0) < c:
                b.w[s] = c

    def dma(self, out, in_, slot, reads=(), writes=(), accw=(), q="sp"):
        if hasattr(slot, "b"):
            slot = slot.b
        if slot.dsem is None:
            slot.dsem = self.dsems[self.dnext]
            self.dnext += 1
        deps = {}
        for b in reads:
            self._add(deps, b.w)
        for b in writes:
            self._add(deps, b.w)
            self._add(deps, b.r)
        for b in accw:
            self._add(deps, b.r)
            if not b.name.startswith("DRAM:"):
                self._add(deps, b.w)
        self._wait(q, deps)
        s = slot.dsem
        self.cnt[s] += 16
        c = self.cnt[s]
        self.q[q].append(("i", lambda e, o=out, i=in_: e.dma_start(out=o, in_=i), s, 16))
        self.nins += 1
        for b in reads:
            if b.r.get(s, 0) < c:
                b.r[s] = c
        for b in writes:
            b.w = {s: c}
            b.r = {}
        for b in accw:
            if b.w.get(s, 0) < c:
                b.w[s] = c

    def barrier(self, bufs=()):
        allc = {s: c for s, c in self.cnt.items() if c > 0}
        for E in self.ENG:
            self._wait(E, allc)
        if self.reuse_dsem:
            for b in bufs:
                b.dsem = None
            self.dnext = 0

    def prune_incs(self):
        eng_sems = ("pe", "act", "dve", "pool")
        targets = {s: set() for s in eng_sems}
        for e in self.ENG:
            for a in self.q[e]:
                if a[0] == "w" and a[1] in targets:
                    targets[a[1]].add(a[2])
        rank = {}
        for s in eng_sems:
            rank[s] = {c: i + 1 for i, c in enumerate(sorted(targets[s]))}
        for e in self.ENG:
            newq = []
            cnt = 0
            for a in self.q[e]:
                if a[0] == "w":
                    if a[1] in rank:
                        newq.append(("w", a[1], rank[a[1]][a[2]]))
                    else:
                        newq.append(a)
                else:
                    if a[2] in rank and a[2] == e:
                        cnt += 1
                        if cnt in rank[e]:
                            newq.append(a)
                        else:
                            newq.append(("i", a[1], None, 0))
                    else:
                        newq.append(a)
            self.q[e] = newq

    def check_deadlock(self):
        pos = {e: 0 for e in self.ENG}
        val = {s: 0 for s in self.cnt}
        progress = True
        while progress:
            progress = False
            for e in self.ENG:
                q = self.q[e]
                while pos[e] < len(q):
                    a = q[pos[e]]
                    if a[0] == "w":
                        if val[a[1]] >= a[2]:
                            pos[e] += 1
                            progress = True
                        else:
                            break
                    else:
                        if a[2] is not None:
                            val[a[2]] += a[3]
                        pos[e] += 1
                        progress = True
        stuck = {e: (pos[e], len(self.q[e]), self.q[e][pos[e]] if pos[e] < len(self.q[e]) else None) for e in self.ENG}
        ok = all(pos[e] == len(self.q[e]) for e in self.ENG)
        return ok, stuck, val

    def emit(self):
        def body(name):
            def f(e):
                for a in self.q[name]:
                    if a[0] == "w":
                        e.wait_ge(self.sems[a[1]], a[2])
                    else:
                        ins = a[1](e)
                        if a[2] is not None:
                            ins.then_inc(self.sems[a[2]], a[3])
            return f
        with self.nc.Block() as block:
            block.sync(body("sp"))
            block.tensor(body("pe"))
            block.scalar(body("act"))
            block.vector(body("dve"))
            block.gpsimd(body("pool"))


def V(t, p0, pn, off, dims):
    base = t[:]
    ps = base.ap[0][0]
    return bass.AP(base.tensor, p0 * ps + off, [[ps, pn]] + [list(d) for d in dims])


class Tile:
    def __init__(self, prog, name, shape, dtype, psum=False):
        nc = prog.nc
        if psum:
            self.t = prog.stack_cur.enter_context(nc.psum_tensor(name, list(shape), dtype))
        else:
            self.t = prog.stack_cur.enter_context(nc.sbuf_tensor(name, list(shape), dtype))
        self.b = Buf(name)
        self.shape = shape

    def __getitem__(self, k):
        return self.t[k]


def host_consts(TP):
    i = np.arange(128)
    cs = {}
    cs["ident"] = np.eye(128, dtype=np.float32)
    t = np.arange(C)
    UI = (t[:, None] <= t[None, :]).astype(np.float32)
    LI = (t[:, None] >= t[None, :]).astype(np.float32)
    SL = (t[:, None] > t[None, :]).astype(np.float32)
    SU = (t[:, None] < t[None, :]).astype(np.float32)

    def rep2(m):
        return np.concatenate([m, m], axis=0)
    cs["UI"] = rep2(UI); cs["LI"] = rep2(LI); cs["SL"] = rep2(SL); cs["SU"] = rep2(SU)
    cs["NEGf"] = rep2((1 - UI) * NEG)
    cs["NEGb"] = rep2((1 - LI) * NEG)
    cs["I64"] = rep2(np.eye(C, dtype=np.float32))
    for k in range(6):
        b = 1 << k
        bi = t // b
        m = ((bi[:, None] % 2 == 1) & (bi[None, :] == bi[:, None] - 1)).astype(np.float32)
        cs["MPf%d" % k] = rep2(np.concatenate([m, m.T], axis=1))
        cs["MPb%d" % k] = rep2(np.concatenate([m.T, m], axis=1))
    cs["II"] = rep2(np.concatenate([np.eye(C), np.eye(C)], axis=1).astype(np.float32))
    cs["REL"] = (i[None, :] - i[:, None]).astype(np.float32)
    cs["UI128"] = (i[:, None] <= i[None, :]).astype(np.float32)
    cs["LI128"] = (i[:, None] >= i[None, :]).astype(np.float32)
    cs["IP1"] = np.broadcast_to((i + 1.0)[None, :], (128, 128)).astype(np.float32)
    cs["IREV"] = np.broadcast_to((CR - i).astype(np.float32)[None, :], (128, 128)).copy()
    cs["CJ"] = np.stack([(CR - 1.0 - i), i * 1.0], axis=1).astype(np.float32)
    names = list(cs.keys())
    offs = {}
    cols = 0
    for n in names:
        offs[n] = (cols, cs[n].shape[1])
        cols += cs[n].shape[1]
    arr = np.zeros((128, cols), np.float32)
    for n in names:
        o, w = offs[n]
        arr[:, o:o + w] = cs[n]
    inv = (10000.0 ** (-np.arange(128, dtype=np.float32) / 128.0)).astype(np.float32)
    pos = np.arange(TP, dtype=np.float32)
    ang = (pos[None, :] * inv[:, None]).astype(np.float32)
    trig = np.stack([np.cos(ang), np.sin(ang)], axis=1).astype(np.float32)
    return arr, offs, trig


class Ctx:
    pass


def build_program(TREAL, NSEQ, NL, debug=False, stages="0ABCDEF"):
    SEQ = TREAL - N_META
    TP = ((TREAL + 127) // 128) * 128
    NT = TP // 128
    NCH = TP // C
    nc = bass.Bass("TRN2", target_bir_lowering=False)
    consts_np, coffs, trig_np = host_consts(TP)
    NCONST = consts_np.shape[1]

    def din(name, shape, dt=F32):
        return nc.dram_tensor(name, list(shape), dt, kind="ExternalInput").ap()

    x = din("x", [NSEQ, SEQ, D])
    meta = din("meta_tokens", [N_META, D])
    norm_mix = din("norm_mix", [NL, D])
    w_in = din("w_in", [NL, D, NIN])
    conv_w = din("conv_w", [NL, 5, 3072])
    a_log = din("gdn_a_log", [NL, 16])
    dt_bias = din("gdn_dt_bias", [NL, 16])
    gdn_norm = din("gdn_norm", [NL, 128])
    ret_logit = din("ret_decay_logit", [NL, 8])
    w_up_a = din("w_up_a", [NL, 1024, D])
    w_up_b = din("w_up_b", [NL, 2048, D])
    w_out = din("w_out", [NL, D, D])
    norm_ffn = din("norm_ffn", [NL, D])
    w_ffn_in = din("w_ffn_in", [NL, D, 2 * FFN])
    w_ffn_out = din("w_ffn_out", [NL, FFN, D])
    norm_final = din("norm_final", [1, D])
    consts = din("consts", [128, NCONST])
    trig = din("trig", [128, 2, TP])
    out = nc.dram_tensor("out", [NSEQ, SEQ, D], F32, kind="ExternalOutput").ap()

    skind = "ExternalOutput" if debug else "Internal"

    def dscr(name, shape, dt):
        return nc.dram_tensor(name, list(shape), dt, kind=skind).ap()

    hT = dscr("hT", [NSEQ, 128, 8, TP], F32)
    QaT = dscr("QaT", [NSEQ, 128, 8, TP], BF16)
    KaT = dscr("KaT", [NSEQ, 128, 8, TP], BF16)
    Ka = dscr("Ka", [NSEQ, TP, 1024], BF16)
    Va = dscr("Va", [NSEQ, TP, 1024], BF16)
    Za = dscr("Za", [NSEQ, TP, 1024], BF16)
    AB = dscr("AB", [NSEQ, TP, 32], F32)
    GB = dscr("GB", [NSEQ, TP, 32], F32)
    QbT = dscr("QbT", [NSEQ, 128, 8, TP], BF16)
    KbT = dscr("KbT", [NSEQ, 128, 8, TP], BF16)
    Kb = dscr("Kb", [NSEQ, TP, 1024], BF16)
    Vb = dscr("Vb", [NSEQ, TP, 2048], BF16)
    Gb = dscr("Gb", [NSEQ, TP, 2048], BF16)
    GaT = dscr("GaT", [NSEQ, 128, 8, TP], BF16)
    GbT = dscr("GbT", [NSEQ, 128, 8, TP], BF16)
    Oa = [dscr("Oa%d" % d, [NSEQ, TP, 1024], BF16) for d in range(2)]
    Ob = [dscr("Ob%d" % d, [NSEQ, TP, 2048], BF16) for d in range(2)]
    dbufs = {n: Buf("DRAM:" + n) for n in ("hT", "QaT", "KaT", "Ka", "Va", "Za", "AB", "GB", "QbT", "KbT", "Kb",
                                 "Vb", "Gb", "GaT", "GbT", "Oa0", "Oa1", "Ob0", "Ob1", "out")}

    dbgs = {}

    def dbg(P, name, tl, ap, shape, dt):
        if not debug or name in dbgs:
            return
        dbgs[name] = nc.dram_tensor("dbg_" + name, list(shape), dt, kind="ExternalOutput").ap()
        bb = Buf("dbg_" + name)
        P.dma(dbgs[name], ap, bb, reads=[tl.b], writes=[bb])

    with ExitStack() as gstack:
        P = Prog(nc, gstack, n_dsem=60)

        uid = [0]

        def tile(stack, name, shape, dt):
            uid[0] += 1
            name = "%s_u%d" % (name, uid[0])
            t = stack.enter_context(nc.sbuf_tensor(name, list(shape), dt))
            tl = Ctx()
            tl.t = t
            tl.b = Buf(name)
            return tl

        PS = []
        for k in range(8):
            tl = Ctx()
            tl.t = gstack.enter_context(nc.psum_tensor("ps%d" % k, [128, 512], F32))
            tl.b = Buf("ps%d" % k)
            PS.append(tl)
        psn = [0]

        def bank():
            k = psn[0]
            psn[0] = (k + 1) % 8
            return PS[k]

        cst = tile(gstack, "cst", [128, NCONST], F32)
        identb = tile(gstack, "identb", [128, 128], BF16)
        onesf = tile(gstack, "onesf", [128, 128], F32)
        onesb = tile(gstack, "onesb", [128, 128], BF16)
        P.dma(cst.t[:], consts[:, :], cst.b, writes=[cst.b])

        def cs(name, p0=0, pn=128):
            o, w = coffs[name]
            return cst.t[p0:p0 + pn, o:o + w]

        def csv(name, p0, pn, dims, off=0):
            o, w = coffs[name]
            return V(cst.t, p0, pn, o + off, dims)

        P.op("dve", lambda e: e.tensor_copy(out=identb.t[:], in_=cs("ident")), reads=[cst.b], writes=[identb.b])
        P.op("pool", lambda e: e.memset(onesf.t[:], 1.0), writes=[onesf.b])
        P.op("pool", lambda e: e.memset(onesb.t[:], 1.0), writes=[onesb.b])
        identf = cs("ident")
        epsD = tile(gstack, "epsD", [128, 4], F32)
        for j, v in enumerate((D * EPS, EPS, 128.0 * EPS, 1.0)):
            P.op("pool", lambda e, j=j, v=v: e.memset(epsD.t[:, j:j + 1], float(v)), writes=[epsD.b] if j == 0 else [], accw=[] if j == 0 else [epsD.b])

        def stage0():
            with ExitStack() as st:
                xt = [tile(st, "xt%d" % i, [128, D], F32) for i in range(2)]
                ho = [tile(st, "ho%d" % i, [128, 8, 128], F32) for i in range(2)]
                it = 0
                for s in range(NSEQ):
                    for n in range(NT):
                        xb = xt[it % 2]
                        hb = ho[it % 2]
                        it += 1
                        t0 = n * 128
                        lo = max(t0, N_META)
                        hi = min(t0 + 128, TREAL)
                        full = (t0 >= N_META) and (t0 + 128 <= TREAL)
                        if not full:
                            P.op("pool", lambda e, xb=xb: e.memset(xb.t[:], 0.0), writes=[xb.b])
                        if t0 < N_META:
                            P.dma(xb.t[0:N_META, :], meta[:, :], xb, accw=[xb.b])
                        if hi > lo:
                            if full:
                                P.dma(xb.t[:, :], x[s, lo - N_META:hi - N_META, :], xb, writes=[xb.b])
                            else:
                                P.dma(xb.t[lo - t0:hi - t0, :], x[s, lo - N_META:hi - N_META, :], xb, accw=[xb.b])
                        for half in range(2):
                            pb = bank()
                            for cc in range(4):
                                c = half * 4 + cc
                                P.op("pe", lambda e, pb=pb, xb=xb, c=c, cc=cc: e.transpose(
                                    out=pb.t[:, cc * 128:(cc + 1) * 128], in_=xb.t[:, c * 128:(c + 1) * 128],
                                    identity=identf), reads=[xb.b, cst.b], writes=[pb.b], inc=(cc == 3))
                            eng = "act" if half == 0 else "dve"
                            if eng == "act":
                                P.op("act", lambda e, pb=pb, hb=hb, half=half: e.copy(
                                    out=hb.t[:, half * 4:half * 4 + 4, :], in_=pb.t[:, :].rearrange("p (c t) -> p c t", c=4)),
                                    reads=[pb.b], writes=[hb.b] if half == 0 else [], accw=[] if half == 0 else [hb.b])
                            else:
                                P.op("dve", lambda e, pb=pb, hb=hb, half=half: e.tensor_copy(
                                    out=hb.t[:, half * 4:half * 4 + 4, :], in_=pb.t[:, :].rearrange("p (c t) -> p c t", c=4)),
                                    reads=[pb.b], accw=[hb.b])
                        P.dma(hT[s, :, :, t0:t0 + 128], hb.t[:], hb, reads=[hb.b], accw=[dbufs["hT"]])
                P.barrier([b.b for b in xt + ho])

        def load_vec_T(st, name, src_ap, nrow, scale):
            raw = tile(st, name + "_raw", [nrow, 128], F32)
            res = tile(st, name, [128, nrow], F32)
            P.dma(raw.t[:], src_ap, raw, writes=[raw.b])
            pb = bank()
            P.op("pe", lambda e: e.transpose(out=pb.t[:, 0:nrow], in_=raw.t[:, :], identity=identf[0:nrow, 0:nrow]),
                 reads=[raw.b, cst.b], writes=[pb.b])
            P.op("dve", lambda e: e.tensor_scalar(out=res.t[:], in0=pb.t[:, 0:nrow], scalar1=float(scale), scalar2=None,
                                                  op0=ALU.mult), reads=[pb.b], writes=[res.b])
            return res

        def stageA(l, s):
            with ExitStack() as st:
                HN = tile(st, "HN", [128, 8, TP], BF16)
                gs = load_vec_T(st, "gs", norm_mix[l].rearrange("(c p) -> c p", p=128), 8, 32.0)
                cwr = tile(st, "cwr", [5, 512], F32)
                cw = tile(st, "cw", [128, 24, 5], F32)
                for g4 in range(6):
                    P.dma(cwr.t[:], conv_w[l, :, g4 * 512:(g4 + 1) * 512], cwr, writes=[cwr.b])
                    pb = bank()
                    for j in range(4):
                        blk = g4 * 4 + j
                        P.op("pe", lambda e, pb=pb, j=j, blk=blk: e.transpose(
                            out=pb.t[:, j * 5:j * 5 + 5], in_=cwr.t[:, j * 128:(j + 1) * 128], identity=identf[0:5, 0:5]),
                            reads=[cwr.b, cst.b], writes=[pb.b], inc=(j == 3))
                    P.op("dve", lambda e, pb=pb, g4=g4: e.tensor_copy(
                        out=cw.t[:, g4 * 4:g4 * 4 + 4, :], in_=pb.t[:, 0:20].rearrange("p (j k) -> p j k", k=5)),
                        reads=[pb.b], accw=[cw.b])
                with ExitStack() as st1:
                    hw = [tile(st1, "hw%d" % i, [128, 8, 512], F32) for i in range(2)]
                    sq = tile(st1, "sq", [128, 8, 512], F32)
                    rr = tile(st1, "rr", [128, 512], F32)
                    for wi, w0 in enumerate(range(0, TP, 512)):
                        wn = min(512, TP - w0)
                        h = hw[wi % 2]
                        P.dma(h.t[:, :, 0:wn], hT[s, :, :, w0:w0 + wn], h, reads=[dbufs["hT"]], writes=[h.b])
                        P.op("act", lambda e, h=h, wn=wn: e.activation(out=sq.t[:, :, 0:wn], in_=h.t[:, :, 0:wn], func=AF.Square),
                             reads=[h.b], writes=[sq.b])
                        pb = bank()
                        for c in range(8):
                            P.op("pe", lambda e, pb=pb, c=c, wn=wn: e.matmul(pb.t[:, 0:wn], lhsT=onesf.t[:, :], rhs=sq.t[:, c, 0:wn],
                                                                              start=(c == 0), stop=(c == 7)),
                                 reads=[sq.b, onesf.b], writes=[pb.b], inc=(c == 7))
                        P.op("act", lambda e, pb=pb, wn=wn: e.activation(out=rr.t[:, 0:wn], in_=pb.t[:, 0:wn], func=AF.Ln, bias=epsD.t[:, 0:1]),
                             reads=[pb.b, epsD.b], writes=[rr.b])
                        P.op("act", lambda e, wn=wn: e.activation(out=rr.t[:, 0:wn], in_=rr.t[:, 0:wn], func=AF.Exp, scale=-0.5),
                             reads=[rr.b], writes=[rr.b])
                        P.op("dve", lambda e, h=h, wn=wn, w0=w0: e.tensor_tensor(
                            out=HN.t[:, :, w0:w0 + wn], in0=h.t[:, :, 0:wn], in1=V(rr.t, 0, 128, 0, [(0, 8), (1, wn)]), op=ALU.mult),
                            reads=[h.b, rr.b], accw=[HN.b])
                    P.barrier([b.b for b in hw])
                with ExitStack() as st2:
                    wf = [tile(st2, "wf%d" % i, [128, 8, 128], F32) for i in range(4)]
                    wb = [tile(st2, "wb%d" % i, [128, 8, 128], BF16) for i in range(4)]
                    wbt = [tile(st2, "wbt%d" % i, [128, 8, 512], BF16) for i in range(2)]
                    PRE = tile(st2, "PRE", [128, TP + 4], BF16)
                    dg = tile(st2, "dg", [128, 5, 128], BF16)
                    ACTB = tile(st2, "ACTB", [128, TP], F32)
                    OF = tile(st2, "OF", [128, TP], BF16)
                    RNF = tile(st2, "RNF", [128, TP], F32)
                    ofm = [tile(st2, "ofm%d" % i, [128, 512], BF16) for i in range(4)]
                    otm = [tile(st2, "otm%d" % i, [128, 512], BF16) for i in range(4)]
                    otf = [tile(st2, "otf%d" % i, [128, 32], F32) for i in range(2)]
                    tmp = [tile(st2, "tmp%d" % i, [128, 512], F32) for i in range(4)]
                    trw = [tile(st2, "trw%d" % i, [128, 2, 512], F32) for i in range(2)]
                    ctr = {"w": 0, "ofm": 0, "otm": 0, "act": 0, "tmp": 0, "otf": 0, "wt": 0}

                    def nxt(lst, key):
                        k = ctr[key]
                        ctr[key] = k + 1
                        return lst[k % len(lst)]

                    P.op("pool", lambda e: e.memset(PRE.t[:], 0.0), writes=[PRE.b])
                    wins = [(w0, min(512, TP - w0)) for w0 in range(0, TP, 512)]

                    wjobs = [(blk * 128, 128) for blk in range(24)]
                    for qk in range(2):
                        for hh in range(4):
                            c0 = (O_QB if qk == 0 else O_KB) + hh * 256
                            wjobs += [(c0, 128), (c0 + 128, 128)]
                    for gi in range(2):
                        for blk in range(8):
                            wjobs.append(((O_GA if gi == 0 else O_GBT) + blk * 128, 128))
                    for base, nb in ((O_Z, 2), (O_VB, 4), (O_GB, 4)):
                        for j in range(nb):
                            for j2 in range(4):
                                wjobs.append((base + j * 512 + j2 * 128, 128))
                    wjobs.append((O_A, 32))
                    issued = [0]

                    def issue_upto(n):
                        while issued[0] < min(n, len(wjobs)):
                            i = issued[0]
                            issued[0] += 1
                            c0, ncol = wjobs[i]
                            f, b = wf[i % 4], wb[i % 4]

                            def go(f=f, b=b, c0=c0, ncol=ncol):
                                P.dma(f.t[:, :, 0:ncol], w_in[l, :, c0:c0 + ncol].rearrange("(c p) n -> p c n", p=128), f, writes=[f.b])
                                P.op("pool", lambda e: e.tensor_tensor(out=b.t[:, :, 0:ncol], in0=f.t[:, :, 0:ncol],
                                                                       in1=V(gs.t, 0, 128, 0, [(1, 8), (0, ncol)]), op=ALU.mult),
                                     reads=[f.b, gs.b], writes=[b.b])
                            go()

                    def load_w(c0, ncol=128):
                        i = ctr["w"]
                        ctr["w"] += 1
                        assert wjobs[i] == (c0, ncol), (i, wjobs[i], c0, ncol)
                        issue_upto(i + 3)
                        return wb[i % 4]

                    def fm_mm(b, w0, wn):
                        pb = bank()
                        for c in range(8):
                            P.op("pe", lambda e, pb=pb, c=c: e.matmul(pb.t[:, 0:wn], lhsT=b.t[:, c, :], rhs=HN.t[:, c, w0:w0 + wn],
                                                                       start=(c == 0), stop=(c == 7)),
                                 reads=[b.b, HN.b], writes=[pb.b], inc=(c == 7))
                        return pb

                    def transpose_out(src, wn, dst_ap_fn, w0, src_off=0):
                        o = nxt(otm, "otm")
                        pb = bank()
                        nt = wn // 128
                        pbv = pb.t[:, :].bitcast(BF16)
                        for j in range(nt):
                            P.op("pe", lambda e, j=j: e.transpose(out=pbv[:, j * 128:(j + 1) * 128], in_=src.t[:, src_off + j * 128:src_off + (j + 1) * 128],
                                                                   identity=identb.t[:, :]),
                                 reads=[src.b, identb.b], writes=[pb.b], inc=(j == nt - 1))
                        P.op("act", lambda e: e.copy(out=o.t[:, 0:wn], in_=pbv[:, 0:wn]), reads=[pb.b], writes=[o.b])
                        for j in range(nt):
                            P.dma(dst_ap_fn(w0 + j * 128), o.t[:, j * 128:(j + 1) * 128], o, reads=[o.b], accw=[dst_ap_fn.buf])

                    for blk in range(24):
                        kind = blk // 8
                        hh = blk % 8
                        b = load_w(blk * 128)
                        for k in range(5):
                            P.op("dve", lambda e, k=k, blk=blk: e.tensor_scalar(out=dg.t[:, k, :], in0=identf, scalar1=cw.t[:, blk, k:k + 1],
                                                                               scalar2=None, op0=ALU.mult),
                                 reads=[cst.b, cw.b], writes=[dg.b] if k == 0 else [], accw=[] if k == 0 else [dg.b])
                        for (w0, wn) in wins:
                            pb = fm_mm(b, w0, wn)
                            P.op("act", lambda e, pb=pb, w0=w0, wn=wn: e.copy(out=PRE.t[:, 2 + w0:2 + w0 + wn], in_=pb.t[:, 0:wn]),
                                 reads=[pb.b], accw=[PRE.b])
                        for (w0, wn) in wins:
                            pb = bank()
                            for k in range(5):
                                P.op("pe", lambda e, pb=pb, k=k, w0=w0, wn=wn: e.matmul(pb.t[:, 0:wn], lhsT=dg.t[:, k, :], rhs=PRE.t[:, w0 + k:w0 + k + wn],
                                                                                         start=(k == 0), stop=(k == 4)),
                                     reads=[dg.b, PRE.b], writes=[pb.b], inc=(k == 4))
                            P.op("act", lambda e, pb=pb, w0=w0, wn=wn: e.activation(out=ACTB.t[:, w0:w0 + wn], in_=pb.t[:, 0:wn], func=AF.Silu),
                                 reads=[pb.b], writes=[ACTB.b] if w0 == 0 else [], accw=[] if w0 == 0 else [ACTB.b])
                        if TREAL < TP:
                            P.op("dve", lambda e: e.memset(ACTB.t[:, TREAL:TP], 0.0), accw=[ACTB.b])
                        if kind < 2:
                            sc = math.sqrt(128.0) if kind == 0 else 1.0
                            epc = 2 if kind == 0 else 1
                            P.op("act", lambda e, sc=sc: e.activation(out=OF.t[:, :], in_=ACTB.t[:, :], func=AF.Square, scale=sc), reads=[ACTB.b], writes=[OF.b])
                            for (w0, wn) in wins:
                                pb2 = bank()
                                P.op("pe", lambda e, pb2=pb2, w0=w0, wn=wn: e.matmul(pb2.t[:, 0:wn], lhsT=onesb.t[:, :], rhs=OF.t[:, w0:w0 + wn], start=True, stop=True),
                                     reads=[OF.b, onesb.b], writes=[pb2.b])
                                P.op("act", lambda e, pb2=pb2, w0=w0, wn=wn, epc=epc: e.activation(out=RNF.t[:, w0:w0 + wn], in_=pb2.t[:, 0:wn], func=AF.Ln, bias=epsD.t[:, epc:epc + 1]),
                                     reads=[pb2.b, epsD.b], writes=[RNF.b] if w0 == 0 else [], accw=[] if w0 == 0 else [RNF.b])
                            P.op("act", lambda e: e.activation(out=RNF.t[:, :], in_=RNF.t[:, :], func=AF.Exp, scale=-0.5), reads=[RNF.b], writes=[RNF.b])
                            P.op("dve", lambda e: e.tensor_tensor(out=OF.t[:, :], in0=ACTB.t[:, :], in1=RNF.t[:, :], op=ALU.mult), reads=[ACTB.b, RNF.b], writes=[OF.b])
                            dstT = QaT if kind == 0 else KaT
                            P.dma(dstT[s, :, hh, :], OF.t[:, :], OF, reads=[OF.b], accw=[dbufs["QaT" if kind == 0 else "KaT"]])
                        else:
                            P.op("dve", lambda e: e.tensor_copy(out=OF.t[:, :], in_=ACTB.t[:, :]), reads=[ACTB.b], writes=[OF.b])
                        if kind >= 1:
                            dst = Ka if kind == 1 else Va

                            def dfn(t0, dst=dst, hh=hh):
                                return dst[s, t0:t0 + 128, hh * 128:(hh + 1) * 128]
                            dfn.buf = dbufs["Ka" if kind == 1 else "Va"]
                            for (w0, wn) in wins:
                                transpose_out(OF, wn, dfn, w0, src_off=w0)

                    for qk in range(2):
                        for hh in range(4):
                            c0 = (O_QB if qk == 0 else O_KB) + hh * 256
                            b1 = load_w(c0)
                            b2 = load_w(c0 + 128)
                            for (w0, wn) in wins:
                                p1 = fm_mm(b1, w0, wn)
                                p2 = fm_mm(b2, w0, wn)
                                trg = trw[ctr["tmp"] % 2]
                                P.dma(trg.t[:, :, 0:wn], trig[:, :, w0:w0 + wn], trg, writes=[trg.b])
                                cosv = trg.t[:, 0, 0:wn]
                                sinv = trg.t[:, 1, 0:wn]
                                t1, t2, t3, t4 = [nxt(tmp, "tmp") for _ in range(4)]
                                o1 = nxt(ofm, "ofm")
                                o2 = nxt(ofm, "ofm")
                                for (tt, pp, tr) in ((t1, p1, cosv), (t2, p2, sinv), (t3, p1, sinv), (t4, p2, cosv)):
                                    P.op("dve", lambda e, tt=tt, pp=pp, tr=tr, wn=wn: e.tensor_tensor(out=tt.t[:, 0:wn], in0=pp.t[:, 0:wn], in1=tr, op=ALU.mult),
                                         reads=[pp.b, trg.b], writes=[tt.b])
                                    ctr["tmp"] += 0
                                P.op("pool", lambda e, o1=o1, t1=t1, t2=t2, wn=wn: e.tensor_tensor(out=o1.t[:, 0:wn], in0=t1.t[:, 0:wn], in1=t2.t[:, 0:wn], op=ALU.subtract),
                                     reads=[t1.b, t2.b], writes=[o1.b])
                                P.op("pool", lambda e, o2=o2, t3=t3, t4=t4, wn=wn: e.tensor_tensor(out=o2.t[:, 0:wn], in0=t3.t[:, 0:wn], in1=t4.t[:, 0:wn], op=ALU.add),
                                     reads=[t3.b, t4.b], writes=[o2.b])
                                dstT = QbT if qk == 0 else KbT
                                nm = "QbT" if qk == 0 else "KbT"
                                P.dma(dstT[s, :, 2 * hh, w0:w0 + wn], o1.t[:, 0:wn], o1, reads=[o1.b], accw=[dbufs[nm]])
                                P.dma(dstT[s, :, 2 * hh + 1, w0:w0 + wn], o2.t[:, 0:wn], o2, reads=[o2.b], accw=[dbufs[nm]])
                                if qk == 1:
                                    for half, oo in ((0, o1), (1, o2)):
                                        def dfn(t0, hh=hh, half=half):
                                            return Kb[s, t0:t0 + 128, hh * 256 + half * 128:hh * 256 + half * 128 + 128]
                                        dfn.buf = dbufs["Kb"]
                                        transpose_out(oo, wn, dfn, w0)

                    for gi in range(2):
                        for blk in range(8):
                            c0 = (O_GA if gi == 0 else O_GBT) + blk * 128
                            b = load_w(c0)
                            for (w0, wn) in wins:
                                pb = fm_mm(b, w0, wn)
                                tt = nxt(tmp, "tmp")
                                o = nxt(ofm, "ofm")
                                P.op("act", lambda e, pb=pb, tt=tt, wn=wn: e.activation(out=tt.t[:, 0:wn], in_=pb.t[:, 0:wn], func=AF.Tanh, scale=0.5),
                                     reads=[pb.b], writes=[tt.b])
                                P.op("dve", lambda e, tt=tt, o=o, wn=wn: e.tensor_scalar(out=o.t[:, 0:wn], in0=tt.t[:, 0:wn], scalar1=0.5, scalar2=0.5,
                                                                                        op0=ALU.mult, op1=ALU.add),
                                     reads=[tt.b], writes=[o.b])
                                dstT = GaT if gi == 0 else GbT
                                P.dma(dstT[s, :, blk, w0:w0 + wn], o.t[:, 0:wn], o, reads=[o.b], accw=[dbufs["GaT" if gi == 0 else "GbT"]])

                    def tm_block(c0, ncol, epi):
                        i = ctr["wt"]
                        ctr["wt"] += 1
                        bt = wbt[i % 2]
                        for j in range(0, ncol, 128):
                            nj = min(128, ncol - j)
                            b = load_w(c0 + j, nj)
                            P.op("pool", lambda e, b=b, j=j, nj=nj: e.tensor_copy(out=bt.t[:, :, j:j + nj], in_=b.t[:, :, 0:nj]),
                                 reads=[b.b], writes=[bt.b] if j == 0 else [], accw=[] if j == 0 else [bt.b])
                        for n in range(NT):
                            pb = bank()
                            for c in range(8):
                                P.op("pe", lambda e, pb=pb, c=c, n=n: e.matmul(pb.t[:, 0:ncol], lhsT=HN.t[:, c, n * 128:(n + 1) * 128], rhs=bt.t[:, c, 0:ncol],
                                                                                start=(c == 0), stop=(c == 7)),
                                     reads=[bt.b, HN.b], writes=[pb.b], inc=(c == 7))
                            epi(pb, n)

                    def epi_act(func, dst, nm, col0, ncol):
                        def f(pb, n):
                            o = nxt(otm, "otm")
                            P.op("act", lambda e: e.activation(out=o.t[:, 0:ncol], in_=pb.t[:, 0:ncol], func=func), reads=[pb.b], writes=[o.b])
                            P.dma(dst[s, n * 128:(n + 1) * 128, col0:col0 + ncol], o.t[:, 0:ncol], o, reads=[o.b], accw=[dbufs[nm]])
                        return f

                    for j in range(2):
                        tm_block(O_Z + j * 512, 512, epi_act(AF.Silu, Za, "Za", j * 512, 512))
                    for j in range(4):
                        tm_block(O_VB + j * 512, 512, epi_act(AF.Copy, Vb, "Vb", j * 512, 512))
                    for j in range(4):
                        tm_block(O_GB + j * 512, 512, epi_act(AF.Silu, Gb, "Gb", j * 512, 512))

                    def epi_ab(pb, n):
                        o = nxt(otf, "otf")
                        P.op("act", lambda e: e.copy(out=o.t[:, 0:32], in_=pb.t[:, 0:32]), reads=[pb.b], writes=[o.b])
                        P.dma(AB[s, n * 128:(n + 1) * 128, :], o.t[:, 0:32], o, reads=[o.b], accw=[dbufs["AB"]])
                    tm_block(O_A, 32, epi_ab)
                    P.barrier([t.b for t in wf + wb + wbt + [OF] + ofm + otm + otf + tmp + trw])

                with ExitStack() as st3:
                    abt = tile(st3, "abt", [128, NT, 32], F32)
                    gbt = tile(st3, "gbt", [128, NT, 32], F32)
                    ex = tile(st3, "ex", [128, NT, 32], F32)
                    dtb = tile(st3, "dtb", [128, 16], F32)
                    alg = tile(st3, "alg", [128, 16], F32)
                    nega = tile(st3, "nega", [128, 16], F32)
                    P.dma(abt.t[:], AB[s].rearrange("(n p) c -> p n c", p=128), abt, reads=[dbufs["AB"]], writes=[abt.b])
                    P.dma(dtb.t[:], dt_bias[l:l + 1, :].partition_broadcast(128), dtb, writes=[dtb.b])
                    P.dma(alg.t[:], a_log[l:l + 1, :].partition_broadcast(128), alg, writes=[alg.b])
                    P.op("act", lambda e: e.activation(out=nega.t[:], in_=alg.t[:], func=AF.Exp), reads=[alg.b], writes=[nega.b])
                    P.op("dve", lambda e: e.tensor_scalar(out=nega.t[:], in0=nega.t[:], scalar1=-1.0, scalar2=None, op0=ALU.mult),
                         reads=[nega.b], writes=[nega.b])
                    P.op("dve", lambda e: e.tensor_tensor(out=ex.t[:, :, 0:16], in0=abt.t[:, :, 0:16], in1=V(dtb.t, 0, 128, 0, [(0, NT), (1, 16)]), op=ALU.add),
                         reads=[abt.b, dtb.b], writes=[ex.b])
                    P.op("act", lambda e: e.activation(out=ex.t[:, :, 0:16], in_=ex.t[:, :, 0:16], func=AF.Exp), reads=[ex.b], writes=[ex.b])
                    P.op("act", lambda e: e.activation(out=ex.t[:, :, 0:16], in_=ex.t[:, :, 0:16], func=AF.Ln, bias=epsD.t[:, 3:4]), reads=[ex.b, epsD.b], writes=[ex.b])
                    P.op("dve", lambda e: e.tensor_tensor(out=gbt.t[:, :, 0:16], in0=ex.t[:, :, 0:16], in1=V(nega.t, 0, 128, 0, [(0, NT), (1, 16)]), op=ALU.mult),
                         reads=[ex.b, nega.b], writes=[gbt.b])
                    P.op("act", lambda e: e.activation(out=ex.t[:, :, 16:32], in_=abt.t[:, :, 16:32], func=AF.Exp, scale=-1.0), reads=[abt.b], writes=[ex.b])
                    P.op("act", lambda e: e.activation(out=ex.t[:, :, 16:32], in_=ex.t[:, :, 16:32], func=AF.Ln, bias=epsD.t[:, 3:4]), reads=[ex.b, epsD.b], writes=[ex.b])
                    P.op("act", lambda e: e.activation(out=gbt.t[:, :, 16:32], in_=ex.t[:, :, 16:32], func=AF.Exp, scale=-0.5), reads=[ex.b], accw=[gbt.b])
                    P.dma(GB[s].rearrange("(n p) c -> p n c", p=128), gbt.t[:], gbt, reads=[gbt.b], writes=[dbufs["GB"]])
                    P.barrier([abt.b, gbt.b, dtb.b, alg.b])
                P.barrier([HN.b, cwr.b])


        def run_rr(gens):
            gens = list(gens)
            while gens:
                for g in list(gens):
                    try:
                        next(g)
                    except StopIteration:
                        gens.remove(g)

        def stageB(l):
            with ExitStack() as st:
                GRP = 1
                ngrp = (NCH + GRP - 1) // GRP
                allb = []

                def mk(name, shape, dt, n=1):
                    r = [tile(st, "%s_%d" % (name, i), shape, dt) for i in range(n)]
                    allb.extend(r)
                    return r

                NEG8g = [mk("NEG8g%d" % dd, [C, 8, C], F32)[0] for dd in range(2)]
                for dd in range(2):
                    P.op("dve", lambda e, dd=dd: e.tensor_copy(out=NEG8g[dd].t[:], in_=csv("NEGf" if dd == 0 else "NEGb", 0, C, [(0, 8), (1, C)])),
                         reads=[cst.b], writes=[NEG8g[dd].b])

                def chain(s, d, cidx):
                    rg = [0]

                    def fbank():
                        k = rg[0]
                        rg[0] = (k + 1) % 2
                        return PS[2 * cidx + k]
                    bbank = fbank
                    AI = "UI" if d == 0 else "LI"
                    SA = "SL" if d == 0 else "SU"
                    MPn = "MPf" if d == 0 else "MPb"
                    glc = C - 1 if d == 0 else 0
                    nm = "c%d" % cidx
                    qTg = mk(nm + "qTg", [128, 8, GRP * C], BF16, 2); kTg = mk(nm + "kTg", [128, 8, GRP * C], BF16, 2)
                    kg = mk(nm + "kg", [C, GRP, 1024], BF16) * 2; vg = mk(nm + "vg", [C, GRP, 1024], BF16) * 2
                    gbg = mk(nm + "gbg", [C, GRP, 32], F32, 2)
                    NEG8 = NEG8g[d]
                    S = mk(nm + "S", [128, 8, 128], F32)[0]; Sb = mk(nm + "Sb", [128, 8, 128], BF16)[0]
                    rhsA = mk(nm + "rhsA", [C, 8, C], F32)[0]; rhsC = rhsA
                    EGCr = mk(nm + "EGCr", [128, 8, C], F32)[0]; DTm = mk(nm + "DTm", [C, 8, C], F32)[0]
                    EG = mk(nm + "EG", [C, 16], F32)[0]; scg = mk(nm + "scg", [C, 8], F32)[0]
                    KhT = mk(nm + "KhT", [128, 8, C], BF16)[0]; Kg = mk(nm + "Kg", [C, 8, 128], BF16)[0]; Gh = mk(nm + "Gh", [C, 8, C], BF16)[0]
                    DEa = mk(nm + "DEa", [C, 8, 2 * C], BF16)[0]; DEb = mk(nm + "DEb", [C, 8, 2 * C], BF16)[0]
                    DE32a = mk(nm + "DE32a", [C, 8, 2 * C], F32)[0]; DE32b = DE32a
                    T1m = mk(nm + "T1m", [C, 8, 2 * C], BF16)[0]
                    X = mk(nm + "X", [C, 8, C], BF16) * 2; nWT = mk(nm + "nWT", [128, 8, C], BF16) * 2; PT = mk(nm + "PT", [C, 8, C], BF16) * 2
                    QdT = mk(nm + "QdT", [128, 8, C], BF16) * 2; Kd = mk(nm + "Kd", [C, 8, 128], BF16) * 2; Vt = mk(nm + "Vt", [C, 8, 128], BF16) * 2
                    egl = mk(nm + "egl", [128, 8], F32) * 2; bcol = mk(nm + "bcol", [C, 8], F32) * 2
                    Vn = mk(nm + "Vn", [C, 8, 128], BF16)[0]; Oo = mk(nm + "Oo", [C, 8, 128], BF16) * 2
                    P.op("pool", lambda e: e.memset(S.t[:], 0.0), writes=[S.b])
                    P.op("pool", lambda e: e.memset(Sb.t[:], 0.0), writes=[Sb.b])
                    order = list(range(NCH)) if d == 0 else list(range(NCH - 1, -1, -1))
                    gorder = list(range(ngrp)) if d == 0 else list(range(ngrp - 1, -1, -1))
                    gpos = {g: i for i, g in enumerate(gorder)}

                    def load_group(g, parts=(0, 1)):
                        bi = gpos[g] % 2
                        c0 = g * GRP
                        ncg = min(GRP, NCH - c0)
                        t0 = c0 * C
                        tn = ncg * C
                        q_, k_, kk_, vv_, gb_ = qTg[bi], kTg[bi], kg[bi], vg[bi], gbg[bi]
                        if 0 in parts:
                            P.dma(q_.t[:, :, 0:tn], QaT[s, :, :, t0:t0 + tn], q_, reads=[dbufs["QaT"]], writes=[q_.b])
                            P.dma(k_.t[:, :, 0:tn], KaT[s, :, :, t0:t0 + tn], k_, reads=[dbufs["KaT"]], writes=[k_.b])
                            P.dma(gb_.t[:, 0:ncg, :], GB[s, t0:t0 + tn, :].rearrange("(n p) f -> p n f", p=C), gb_, reads=[dbufs["GB"]], writes=[gb_.b])
                        if 1 in parts:
                            P.dma(kk_.t[:, 0:ncg, :], Ka[s, t0:t0 + tn, :].rearrange("(n p) f -> p n f", p=C), kk_, reads=[dbufs["Ka"]], writes=[kk_.b])
                            P.dma(vv_.t[:, 0:ncg, :], Va[s, t0:t0 + tn, :].rearrange("(n p) f -> p n f", p=C), vv_, reads=[dbufs["Va"]], writes=[vv_.b])

                    def front(i):
                        c = order[i]
                        g = c // GRP
                        ci = c % GRP
                        hb = i % 2
                        first_of_group = (i == 0) or (order[i - 1] // GRP != g)
                        if i == 0:
                            load_group(g)
                        if first_of_group and gpos[g] + 1 < ngrp:
                            load_group(gorder[gpos[g] + 1], parts=(0,))
                        bi = gpos[g] % 2
                        q_, k_, kk_, vv_, gb_ = qTg[bi], kTg[bi], kg[bi], vg[bi], gbg[bi]
                        X_, nW_, PT_, Qd_, Kd_, Vt_, egl_, bc_ = X[hb], nWT[hb], PT[hb], QdT[hb], Kd[hb], Vt[hb], egl[hb], bcol[hb]
                        g_bc = V(gb_.t, 0, C, ci * 32 + d * 8, [(1, 8), (0, C)])
                        b_bc = V(gb_.t, 0, C, ci * 32 + 16 + d * 8, [(1, 8), (0, C)])
                        b_bc128 = V(gb_.t, 0, C, ci * 32 + 16 + d * 8, [(1, 8), (0, 128)])
                        P.op("pool", lambda e: e.tensor_tensor(out=rhsA.t[:], in0=g_bc, in1=csv(AI, 0, C, [(0, 8), (1, C)]), op=ALU.mult),
                             reads=[gb_.b, cst.b], writes=[rhsA.b])
                        P.op("act", lambda e: e.copy(out=bc_.t[:, :], in_=gb_.t[:, ci, 16 + d * 8:24 + d * 8]), reads=[gb_.b], writes=[bc_.b])
                        yield
                        bA, bB = fbank(), fbank()
                        A2 = V(rhsA.t, 0, C, 0, [(1, 8 * C)])
                        P.op("pe", lambda e: e.matmul(bA.t[:, :], lhsT=onesf.t[0:C, :], rhs=A2, start=True, stop=True), reads=[rhsA.b, onesf.b], writes=[bA.b])
                        P.op("pe", lambda e: e.matmul(bB.t[0:C, :], lhsT=cs(SA, 0, C), rhs=A2, start=True, stop=False), reads=[rhsA.b, cst.b], writes=[bB.b], inc=False)
                        P.op("pe", lambda e: e.matmul(bB.t[0:C, :], lhsT=cs("I64", 0, C), rhs=V(NEG8.t, 0, C, 0, [(1, 8 * C)]), start=False, stop=True),
                             reads=[NEG8.b, cst.b], writes=[bB.b])
                        yield
                        P.op("pool", lambda e: e.tensor_tensor(out=rhsC.t[:], in0=b_bc, in1=csv("I64", 0, C, [(0, 8), (1, C)]), op=ALU.mult),
                             reads=[gb_.b, cst.b], writes=[rhsC.b])
                        P.op("act", lambda e: e.activation(out=V(EGCr.t, 0, 128, 0, [(1, 8 * C)]), in_=bA.t[:, :], func=AF.Exp), reads=[bA.b], writes=[EGCr.b])
                        P.op("act", lambda e: e.activation(out=V(DTm.t, 0, C, 0, [(1, 8 * C)]), in_=bB.t[0:C, :], func=AF.Exp), reads=[bB.b], writes=[DTm.b])
                        yield
                        bC, bD = fbank(), fbank()
                        g8 = V(gb_.t, 0, C, ci * 32 + d * 8, [(1, 8)])
                        P.op("pe", lambda e: e.matmul(bC.t[0:C, 0:8], lhsT=cs(AI, 0, C), rhs=g8, start=True, stop=True), reads=[gb_.b, cst.b], writes=[bC.b], inc=False)
                        P.op("pe", lambda e: e.matmul(bC.t[0:C, 8:16], lhsT=cs(SA, 0, C), rhs=g8, start=True, stop=True), reads=[gb_.b, cst.b], writes=[bC.b])
                        P.op("pe", lambda e: e.matmul(bD.t[:, :], lhsT=onesf.t[0:C, :], rhs=V(rhsC.t, 0, C, 0, [(1, 8 * C)]), start=True, stop=True),
                             reads=[rhsC.b, onesf.b], writes=[bD.b])
                        yield
                        P.op("act", lambda e: e.activation(out=EG.t[:, :], in_=bC.t[0:C, 0:16], func=AF.Exp), reads=[bC.b], writes=[EG.b])
                        P.op("dve", lambda e: e.tensor_tensor(out=KhT.t[:], in0=k_.t[:, :, ci * C:(ci + 1) * C], in1=bD.t[:, :].rearrange("p (h c) -> p h c", h=8), op=ALU.mult),
                             reads=[k_.b, bD.b], writes=[KhT.b])
                        P.op("pool", lambda e: e.tensor_tensor(out=Qd_.t[:], in0=q_.t[:, :, ci * C:(ci + 1) * C], in1=EGCr.t[:], op=ALU.mult),
                             reads=[q_.b, EGCr.b], writes=[Qd_.b])
                        P.op("act", lambda e: e.copy(out=egl_.t[:, :], in_=V(EGCr.t, 0, 128, glc, [(C, 8)])), reads=[EGCr.b], writes=[egl_.b])
                        yield
                        bF, bE = fbank(), fbank()
                        for h in range(8):
                            P.op("pe", lambda e, h=h: e.matmul(bF.t[0:C, h * C:(h + 1) * C], lhsT=k_.t[:, h, ci * C:(ci + 1) * C], rhs=q_.t[:, h, ci * C:(ci + 1) * C],
                                                               start=True, stop=True), reads=[k_.b, q_.b], writes=[bF.b], inc=(h == 7))
                        for h in range(8):
                            P.op("pe", lambda e, h=h: e.matmul(bE.t[0:C, h * C:(h + 1) * C], lhsT=KhT.t[:, h, :], rhs=KhT.t[:, h, :], start=True, stop=True),
                                 reads=[KhT.b], writes=[bE.b], inc=(h == 7))
                        P.op("dve", lambda e: e.tensor_tensor(out=scg.t[:], in0=gb_.t[:, ci, 16 + d * 8:24 + d * 8], in1=EG.t[:, 0:8], op=ALU.mult),
                             reads=[gb_.b, EG.b], writes=[scg.b])
                        kch = V(kk_.t, 0, C, ci * 1024, [(128, 8), (1, 128)])
                        vch = V(vv_.t, 0, C, ci * 1024, [(128, 8), (1, 128)])
                        P.op("pool", lambda e: e.tensor_tensor(out=Kd_.t[:], in0=kch, in1=V(EG.t, 0, C, 8, [(1, 8), (0, 128)]), op=ALU.mult),
                             reads=[kk_.b, EG.b], writes=[Kd_.b])
                        P.op("pool", lambda e: e.tensor_tensor(out=Vt_.t[:], in0=vch, in1=b_bc128, op=ALU.mult), reads=[vv_.b, gb_.b], writes=[Vt_.b])
                        yield
                        P.op("dve", lambda e: e.tensor_tensor(out=V(PT_.t, 0, C, 0, [(1, 8 * C)]), in0=bF.t[0:C, :], in1=V(DTm.t, 0, C, 0, [(1, 8 * C)]), op=ALU.mult),
                             reads=[bF.b, DTm.b], writes=[PT_.b])
                        P.op("pool", lambda e: e.tensor_tensor(out=Kg.t[:], in0=kch, in1=V(scg.t, 0, C, 0, [(1, 8), (0, 128)]), op=ALU.mult),
                             reads=[kk_.b, scg.b], writes=[Kg.b])
                        if first_of_group and gpos[g] + 1 < ngrp:
                            load_group(gorder[gpos[g] + 1], parts=(1,))
                        P.op("act", lambda e: e.copy(out=V(Gh.t, 0, C, 0, [(1, 8 * C)]), in_=bE.t[0:C, :]), reads=[bE.b], writes=[Gh.b])
                        cur32, oth32, cur, oth = DE32a, DE32b, DEa, DEb
                        P.op("dve", lambda e, cur32=cur32: e.tensor_tensor(out=V(cur32.t, 0, C, 0, [(2 * C, 8), (C, 2), (1, C)]), in0=V(bE.t, 0, C, 0, [(C, 8), (0, 2), (1, C)]),
                                                                           in1=csv(MPn + "0", 0, C, [(0, 8), (C, 2), (1, C)]), op=ALU.mult),
                             reads=[bE.b, cst.b], writes=[cur32.b])
                        P.op("pool", lambda e, cur=cur, cur32=cur32: e.tensor_tensor(out=cur.t[:], in0=csv("II", 0, C, [(0, 8), (1, 2 * C)]), in1=cur32.t[:], op=ALU.subtract),
                             reads=[cur32.b, cst.b], writes=[cur.b])
                        yield
                        for k in range(1, 6):
                            b1 = [fbank(), fbank()]
                            for h in range(8):
                                P.op("pe", lambda e, b1=b1, h=h, cur=cur: e.matmul(b1[h // 4].t[0:C, (h % 4) * 128:(h % 4) * 128 + 128], lhsT=Gh.t[:, h, :], rhs=cur.t[:, h, :],
                                                                                    start=True, stop=True), reads=[Gh.b, cur.b], writes=[b1[h // 4].b], inc=(h % 4 == 3))
                            yield
                            for hf in range(2):
                                P.op("dve", lambda e, b1=b1, hf=hf, k=k: e.tensor_tensor(out=T1m.t[:, hf * 4:hf * 4 + 4, :], in0=b1[hf].t[0:C, :].rearrange("p (h c) -> p h c", h=4),
                                                                                       in1=csv(MPn + str(k), 0, C, [(0, 4), (1, 2 * C)]), op=ALU.mult),
                                     reads=[b1[hf].b, cst.b], writes=[T1m.b] if hf == 0 else [], accw=[] if hf == 0 else [T1m.b])
                            yield
                            b2 = [fbank(), fbank()]
                            for h in range(8):
                                o0 = (h % 4) * 128
                                P.op("pe", lambda e, b2=b2, h=h, cur=cur, o0=o0: e.matmul(b2[h // 4].t[0:C, o0:o0 + C], lhsT=cur.t[:, h, C:2 * C], rhs=T1m.t[:, h, 0:C], start=True, stop=True),
                                     reads=[T1m.b, cur.b], writes=[b2[h // 4].b], inc=False)
                                P.op("pe", lambda e, b2=b2, h=h, cur=cur, o0=o0: e.matmul(b2[h // 4].t[0:C, o0 + C:o0 + 2 * C], lhsT=cur.t[:, h, 0:C], rhs=T1m.t[:, h, C:2 * C], start=True, stop=True),
                                     reads=[T1m.b, cur.b], writes=[b2[h // 4].b], inc=(h % 4 == 3))
                            yield
                            for hf in range(2):
                                P.op("dve", lambda e, b2=b2, hf=hf, cur=cur, oth=oth: e.tensor_tensor(out=oth.t[:, hf * 4:hf * 4 + 4, :], in0=cur.t[:, hf * 4:hf * 4 + 4, :],
                                                                                                    in1=b2[hf].t[0:C, :].rearrange("p (h c) -> p h c", h=4), op=ALU.subtract),
                                     reads=[b2[hf].b, cur.b], writes=[oth.b] if hf == 0 else [], accw=[] if hf == 0 else [oth.b])
                            cur, oth = oth, cur
                            yield
                        P.op("dve", lambda e, cur=cur: e.tensor_tensor(out=X_.t[:], in0=cur.t[:, :, C:2 * C], in1=DTm.t[:], op=ALU.mult),
                             reads=[cur.b, DTm.b], writes=[X_.b])
                        yield
                        bG = fbank()
                        for h in range(8):
                            P.op("pe", lambda e, h=h: e.matmul(bG.t[:, h * C:(h + 1) * C], lhsT=Kg.t[:, h, :], rhs=X_.t[:, h, :], start=True, stop=True),
                                 reads=[Kg.b, X_.b], writes=[bG.b], inc=(h == 7))
                        yield
                        P.op("act", lambda e: e.mul(out=V(nW_.t, 0, 128, 0, [(1, 8 * C)]), in_=bG.t[:, :], mul=-1.0), reads=[bG.b], writes=[nW_.b])
                        yield

                    def back(i):
                        c = order[i]
                        hb = i % 2
                        tok0 = c * C
                        X_, nW_, PT_, Qd_, Kd_, Vt_, egl_, bc_ = X[hb], nWT[hb], PT[hb], QdT[hb], Kd[hb], Vt[hb], egl[hb], bcol[hb]
                        Oo_ = Oo[hb]
                        bV = [bbank(), bbank()]
                        for h in range(8):
                            o0 = (h % 4) * 128
                            P.op("pe", lambda e, h=h, o0=o0: e.matmul(bV[h // 4].t[0:C, o0:o0 + 128], lhsT=X_.t[:, h, :], rhs=Vt_.t[:, h, :], start=True, stop=False),
                                 reads=[X_.b, Vt_.b], writes=[bV[h // 4].b], inc=False)
                            P.op("pe", lambda e, h=h, o0=o0: e.matmul(bV[h // 4].t[0:C, o0:o0 + 128], lhsT=nW_.t[:, h, :], rhs=Sb.t[:, h, :], start=False, stop=True),
                                 reads=[nW_.b, Sb.b], writes=[bV[h // 4].b], inc=(h % 4 == 3))
                        yield
                        for hf in range(2):
                            P.op("dve", lambda e, hf=hf: e.tensor_tensor(out=Vn.t[:, hf * 4:hf * 4 + 4, :], in0=bV[hf].t[0:C, :].rearrange("p (h c) -> p h c", h=4),
                                                                       in1=V(bc_.t, 0, C, hf * 4, [(1, 4), (0, 128)]), op=ALU.mult),
                                 reads=[bV[hf].b, bc_.b], writes=[Vn.b] if hf == 0 else [], accw=[] if hf == 0 else [Vn.b])
                        yield
                        bS = [bbank(), bbank()]
                        for h in range(8):
                            o0 = (h % 4) * 128
                            P.op("pe", lambda e, h=h, o0=o0: e.matmul(bS[h // 4].t[:, o0:o0 + 128], lhsT=Kd_.t[:, h, :], rhs=Vn.t[:, h, :], start=True, stop=True),
                                 reads=[Kd_.b, Vn.b], writes=[bS[h // 4].b], inc=(h % 4 == 3))
                        P.op("pool", lambda e: e.tensor_tensor(out=S.t[:], in0=S.t[:], in1=V(egl_.t, 0, 128, 0, [(1, 8), (0, 128)]), op=ALU.mult),
                             reads=[S.b, egl_.b, Sb.b], writes=[S.b])
                        yield
                        for hf in range(2):
                            P.op("dve", lambda e, hf=hf: e.tensor_tensor(out=S.t[:, hf * 4:hf * 4 + 4, :], in0=S.t[:, hf * 4:hf * 4 + 4, :],
                                                                       in1=bS[hf].t[:, :].rearrange("p (h c) -> p h c", h=4), op=ALU.add),
                                 reads=[bS[hf].b, S.b], writes=[S.b] if hf == 1 else [], accw=[S.b] if hf == 0 else [])
                        yield
                        bO = [bbank(), bbank()]
                        for h in range(8):
                            o0 = (h % 4) * 128
                            P.op("pe", lambda e, h=h, o0=o0: e.matmul(bO[h // 4].t[0:C, o0:o0 + 128], lhsT=Qd_.t[:, h, :], rhs=Sb.t[:, h, :], start=True, stop=False),
                                 reads=[Qd_.b, Sb.b], writes=[bO[h // 4].b], inc=False)
                            P.op("pe", lambda e, h=h, o0=o0: e.matmul(bO[h // 4].t[0:C, o0:o0 + 128], lhsT=PT_.t[:, h, :], rhs=Vn.t[:, h, :], start=False, stop=True),
                                 reads=[PT_.b, Vn.b], writes=[bO[h // 4].b], inc=(h % 4 == 3))
                        P.op("act", lambda e: e.copy(out=Sb.t[:], in_=S.t[:]), reads=[S.b], writes=[Sb.b])
                        yield
                        for hf in range(2):
                            P.op("act", lambda e, hf=hf: e.copy(out=Oo_.t[:, hf * 4:hf * 4 + 4, :], in_=bO[hf].t[0:C, :].rearrange("p (h c) -> p h c", h=4)),
                                 reads=[bO[hf].b], writes=[Oo_.b] if hf == 0 else [], accw=[] if hf == 0 else [Oo_.b])
                        P.dma(Oa[d][s, tok0:tok0 + C, :], V(Oo_.t, 0, C, 0, [(1, 1024)]), Oo_, reads=[Oo_.b], accw=[dbufs["Oa%d" % d]], q="act")
                        yield

                    yield from front(0)
                    for i in range(len(order)):
                        gens = [back(i)]
                        if i + 1 < len(order) and INTERLEAVE_FB:
                            gens.append(front(i + 1))
                        elif i + 1 < len(order):
                            for _ in back(i):
                                yield
                            gens = [front(i + 1)]
                        while gens:
                            for g in list(gens):
                                try:
                                    next(g)
                                    yield
                                except StopIteration:
                                    gens.remove(g)

                chains = []
                for s in range(NSEQ):
                    for d in range(2):
                        chains.append(chain(s, d, len(chains)))
                run_rr(chains)
                P.barrier([t.b for t in allb])

        def rsqrt_act(t_ap, in_ap, scale, epscol, rd, wr):
            P.op("act", lambda e: e.activation(out=t_ap, in_=in_ap, func=AF.Ln, bias=epsD.t[0:t_ap.shape[0], epscol:epscol + 1], scale=float(scale)),
                 reads=rd + [epsD.b], writes=[wr])
            P.op("act", lambda e: e.activation(out=t_ap, in_=t_ap, func=AF.Exp, scale=-0.5), reads=[wr], writes=[wr])

        def stageC(l):
            with ExitStack() as st:
                lgr = tile(st, "lgr", [128, 8], F32)
                lg = tile(st, "lg", [128, 8], F32)
                Dfb = tile(st, "Dfb", [128, 4, 128], F32)
                tmpm = tile(st, "tmpm", [128, 128], F32)
                QFB = [tile(st, "QFB%d" % d, [128, 4, 128], BF16) for d in range(2)]
                KFB = [tile(st, "KFB%d" % d, [128, 4], BF16) for d in range(2)]
                gC = tile(st, "gC", [128, 8], F32)
                def seq_tiles(s):
                    o = Ctx()
                    o.Rs = tile(st, "Rs%d" % s, [128, 4, 2, 512], F32)
                    o.Rb = tile(st, "Rb%d" % s, [128, 4, 2, 512], BF16)
                    o.qT = [tile(st, "qT%d_%d" % (s, i), [128, 8, 128], BF16) for i in range(2)]
                    o.kT = [tile(st, "kT%d_%d" % (s, i), [128, 8, 128], BF16) for i in range(2)]
                    o.kt = [tile(st, "kt%d_%d" % (s, i), [128, 4, 256], BF16) for i in range(2)]
                    o.vt = [tile(st, "vt%d_%d" % (s, i), [128, 2048], BF16) for i in range(2)]
                    o.PTr = [tile(st, "PTr%d_%d" % (s, i), [128, 4, 128], BF16) for i in range(2)]
                    o.Qd = [tile(st, "Qd%d_%d" % (s, i), [128, 8, 128], BF16) for i in range(2)]
                    o.Kdr = [tile(st, "Kdr%d_%d" % (s, i), [128, 4, 256], BF16) for i in range(2)]
                    o.Oo = [tile(st, "Oor%d_%d" % (s, i), [128, 2048], BF16) for i in range(2)]
                    return o
                ST = [seq_tiles(s) for s in range(NSEQ)]
                P.dma(lgr.t[:], ret_logit[l:l + 1, :].partition_broadcast(128), lgr, writes=[lgr.b])
                P.op("act", lambda e: e.activation(out=lg.t[:], in_=lgr.t[:], func=AF.Exp, scale=-1.0), reads=[lgr.b], writes=[lg.b])
                P.op("act", lambda e: e.activation(out=lg.t[:], in_=lg.t[:], func=AF.Ln, bias=epsD.t[:, 3:4]), reads=[lg.b, epsD.b], writes=[lg.b])
                P.op("dve", lambda e: e.tensor_scalar(out=lg.t[:], in0=lg.t[:], scalar1=-1.0, scalar2=None, op0=ALU.mult), reads=[lg.b], writes=[lg.b])
                P.op("dve", lambda e: e.tensor_scalar(out=lgr.t[:], in0=lg.t[:], scalar1=-1.0, scalar2=None, op0=ALU.mult), reads=[lg.b], writes=[lgr.b])
                P.op("act", lambda e: e.activation(out=gC.t[:], in_=lg.t[:], func=AF.Exp, scale=float(CR)), reads=[lg.b], writes=[gC.b])

                def setup_head(h):
                    P.op("act", lambda e: e.activation(out=tmpm.t[:], in_=cs("REL"), func=AF.Exp, scale=lg.t[:, h:h + 1]), reads=[cst.b, lg.b], writes=[tmpm.b])
                    P.op("dve", lambda e: e.tensor_tensor(out=Dfb.t[:, h, :], in0=tmpm.t[:], in1=cs("UI128"), op=ALU.mult), reads=[tmpm.b, cst.b], accw=[Dfb.b])
                    P.op("act", lambda e: e.activation(out=tmpm.t[:], in_=cs("REL"), func=AF.Exp, scale=lgr.t[:, 4 + h:5 + h]), reads=[cst.b, lgr.b, Dfb.b], writes=[tmpm.b])
                    P.op("dve", lambda e: e.tensor_tensor(out=tmpm.t[:], in0=tmpm.t[:], in1=cs("LI128"), op=ALU.mult), reads=[tmpm.b, cst.b], writes=[tmpm.b])
                    P.op("dve", lambda e: e.tensor_tensor(out=Dfb.t[:, h, :], in0=Dfb.t[:, h, :], in1=tmpm.t[:], op=ALU.add), reads=[tmpm.b, Dfb.b], accw=[Dfb.b])
                    P.op("dve", lambda e: e.tensor_scalar(out=Dfb.t[:, h, :], in0=Dfb.t[:, h, :], scalar1=1.0 / 16.0, scalar2=None, op0=ALU.mult), reads=[Dfb.b], accw=[Dfb.b])
                    for d in range(2):
                        P.op("act", lambda e, d=d: e.activation(out=QFB[d].t[:, h, :], in_=cs("IP1" if d == 0 else "IREV"), func=AF.Exp, scale=lg.t[:, d * 4 + h:d * 4 + h + 1]),
                             reads=[cst.b, lg.b], accw=[QFB[d].b])
                        P.op("act", lambda e, d=d: e.activation(out=tmpm.t[:, 0:1], in_=csv("CJ", 0, 128, [(1, 1)], off=d), func=AF.Exp, scale=lg.t[:, d * 4 + h:d * 4 + h + 1]),
                             reads=[cst.b, lg.b], writes=[tmpm.b])
                        P.op("dve", lambda e, d=d: e.tensor_scalar(out=KFB[d].t[:, h:h + 1], in0=tmpm.t[:, 0:1], scalar1=1.0 / 16.0, scalar2=None, op0=ALU.mult),
                             reads=[tmpm.b], accw=[KFB[d].b])
                for h in range(4):
                    setup_head(h)

                def do_chunk(s, d, n, r, cbank):
                    o = ST[s]
                    Rs, Rb = o.Rs, o.Rb
                    t0 = n * 128
                    q_, k_, kk_, v_, PT_, Qd_, Kd_, Oo_ = o.qT[r], o.kT[r], o.kt[r], o.vt[r], o.PTr[r], o.Qd[r], o.Kdr[r], o.Oo[r]
                    P.dma(q_.t[:], QbT[s, :, :, t0:t0 + 128], q_, reads=[dbufs["QbT"]], writes=[q_.b])
                    if d == 0:
                        P.dma(k_.t[:], KbT[s, :, :, t0:t0 + 128], k_, reads=[dbufs["KbT"]], writes=[k_.b])
                    P.dma(V(kk_.t, 0, 128, 0, [(1, 1024)]), Kb[s, t0:t0 + 128, :], kk_, reads=[dbufs["Kb"]], writes=[kk_.b])
                    P.dma(v_.t[:], Vb[s, t0:t0 + 128, :], v_, reads=[dbufs["Vb"]], writes=[v_.b])
                    P.op("pool", lambda e: e.tensor_tensor(out=V(Qd_.t, 0, 128, 0, [(256, 4), (128, 2), (1, 128)]), in0=V(q_.t, 0, 128, 0, [(256, 4), (128, 2), (1, 128)]),
                                                           in1=V(QFB[d].t, 0, 128, 0, [(128, 4), (0, 2), (1, 128)]), op=ALU.mult),
                         reads=[q_.b, QFB[d].b], writes=[Qd_.b])
                    P.op("pool", lambda e: e.tensor_tensor(out=Kd_.t[:], in0=kk_.t[:], in1=V(KFB[d].t, 0, 128, 0, [(1, 4), (0, 256)]), op=ALU.mult),
                         reads=[kk_.b, KFB[d].b], writes=[Kd_.b])
                    if d == 0:
                        pb = cbank()
                        for h in range(4):
                            for hf in range(2):
                                P.op("pe", lambda e, h=h, hf=hf: e.matmul(pb.t[:, h * 128:(h + 1) * 128], lhsT=k_.t[:, 2 * h + hf, :], rhs=q_.t[:, 2 * h + hf, :],
                                                                          start=(hf == 0), stop=(hf == 1)),
                                     reads=[k_.b, q_.b], writes=[pb.b], inc=(h == 3 and hf == 1))
                        P.op("dve", lambda e: e.tensor_tensor(out=V(PT_.t, 0, 128, 0, [(1, 512)]), in0=pb.t[:, :], in1=V(Dfb.t, 0, 128, 0, [(1, 512)]), op=ALU.mult),
                             reads=[pb.b, Dfb.b], writes=[PT_.b])
                    yield
                    for h in range(4):
                        po = cbank()
                        for hf in range(2):
                            P.op("pe", lambda e, h=h, hf=hf, po=po: e.matmul(po.t[:, :], lhsT=Qd_.t[:, 2 * h + hf, :], rhs=Rb.t[:, h, hf, :], start=(hf == 0),
                                                                             stop=(hf == 1 and d == 1)),
                                 reads=[Qd_.b, Rb.b], writes=[po.b], inc=(hf == 1 and d == 1))
                        if d == 0:
                            P.op("pe", lambda e, h=h, po=po: e.matmul(po.t[:, :], lhsT=PT_.t[:, h, :], rhs=v_.t[:, h * 512:(h + 1) * 512], start=False, stop=True),
                                 reads=[PT_.b, v_.b], writes=[po.b])
                        P.op("act", lambda e, h=h, po=po: e.copy(out=Oo_.t[:, h * 512:(h + 1) * 512], in_=po.t[:, :]), reads=[po.b],
                             writes=[Oo_.b] if h == 0 else [], accw=[] if h == 0 else [Oo_.b])
                    P.dma(Ob[d][s, t0:t0 + 128, :], Oo_.t[:], Oo_, reads=[Oo_.b], accw=[dbufs["Ob%d" % d]], q="act")
                    yield
                    for h in range(4):
                        for hf in range(2):
                            pr = cbank()
                            P.op("pe", lambda e, h=h, hf=hf, pr=pr: e.matmul(pr.t[:, :], lhsT=Kd_.t[:, h, hf * 128:(hf + 1) * 128], rhs=v_.t[:, h * 512:(h + 1) * 512],
                                                                             start=True, stop=True),
                                 reads=[Kd_.b, v_.b], writes=[pr.b])
                            P.op("dve", lambda e, h=h, hf=hf, pr=pr: e.scalar_tensor_tensor(out=Rs.t[:, h, hf, :], in0=Rs.t[:, h, hf, :], scalar=gC.t[:, d * 4 + h:d * 4 + h + 1],
                                                                                            in1=pr.t[:, :], op0=ALU.mult, op1=ALU.add),
                                 reads=[pr.b, gC.b, Rs.b, Rb.b], accw=[Rs.b])
                        P.op("act", lambda e, h=h: e.copy(out=Rb.t[:, h, :, :], in_=Rs.t[:, h, :, :]), reads=[Rs.b], accw=[Rb.b])
                        yield

                def chainC(s):
                    rg = [0]

                    def cbank():
                        k = rg[0]
                        rg[0] = (k + 1) % 4
                        return PS[4 * (s % 2) + k]
                    o = ST[s]
                    it = 0
                    for d in range(2):
                        P.op("pool", lambda e: e.memset(o.Rs.t[:], 0.0), writes=[o.Rs.b])
                        P.op("pool", lambda e: e.memset(o.Rb.t[:], 0.0), writes=[o.Rb.b])
                        for n in (range(NT) if d == 0 else range(NT - 1, -1, -1)):
                            yield from do_chunk(s, d, n, it % 2, cbank)
                            it += 1
                run_rr([chainC(s) for s in range(NSEQ)])
                allt = [lgr]
                for o in ST:
                    allt += o.qT + o.kT + o.kt + o.vt + o.Oo
                P.barrier([t.b for t in allt])

        def stageD1(l):
            with ExitStack() as st:
                Wa = tile(st, "Wa", [128, 8, 1024], BF16)
                Wb = tile(st, "Wb", [128, 16, 1024], BF16)
                Wo = tile(st, "Wo", [128, 8, 1024], BF16)
                stg = [tile(st, "stg%d" % i, [128, 1024], F32) for i in range(2)]
                gn = tile(st, "gn", [128, 1], F32)
                P.dma(gn.t[:], gdn_norm[l].rearrange("(p o) -> p o", o=1), gn, writes=[gn.b])
                wi = [0]

                def ldw(src_rows, dst_ap, dstb, first, scal):
                    sg = stg[wi[0] % 2]
                    eng = "dve" if wi[0] % 2 == 0 else "act"
                    wi[0] += 1
                    P.dma(sg.t[:], src_rows, sg, writes=[sg.b])
                    if scal is not None:
                        P.op("dve", lambda e: e.tensor_scalar(out=dst_ap, in0=sg.t[:], scalar1=scal, scalar2=None, op0=ALU.mult), reads=[sg.b, gn.b],
                             writes=[dstb] if first else [], accw=[] if first else [dstb])
                    elif eng == "dve":
                        P.op("dve", lambda e: e.tensor_copy(out=dst_ap, in_=sg.t[:]), reads=[sg.b], writes=[dstb] if first else [], accw=[] if first else [dstb])
                    else:
                        P.op("act", lambda e: e.copy(out=dst_ap, in_=sg.t[:]), reads=[sg.b], writes=[dstb] if first else [], accw=[] if first else [dstb])
                for c in range(8):
                    ldw(w_up_a[l, c * 128:(c + 1) * 128, :], Wa.t[:, c, :], Wa.b, c == 0, gn.t[:, 0:1])
                for c in range(16):
                    ldw(w_up_b[l, c * 128:(c + 1) * 128, :], Wb.t[:, c, :], Wb.b, c == 0, None)
                for c in range(8):
                    ldw(w_out[l, c * 128:(c + 1) * 128, :], Wo.t[:, c, :], Wo.b, c == 0, None)
                WN = 256
                oa0 = tile(st, "oa0", [128, 1024], BF16); oa1 = tile(st, "oa1", [128, 1024], BF16); za = tile(st, "za", [128, 1024], BF16)
                ob0 = tile(st, "ob0", [128, 2048], BF16); ob1 = tile(st, "ob1", [128, 2048], BF16); gbt_ = tile(st, "gbt_", [128, 2048], BF16)
                f1 = tile(st, "f1", [128, 2048], F32); f2 = tile(st, "f2", [128, 2048], F32)
                ssq = tile(st, "ssq", [128, 12], F32)
                oag = [tile(st, "oag%d" % i, [128, 1024], BF16) for i in range(2)]
                obg = [tile(st, "obg%d" % i, [128, 2048], BF16) for i in range(2)]
                oaT = [tile(st, "oaT%d" % i, [128, 8, WN], BF16) for i in range(2)]
                obT = [tile(st, "obT%d" % i, [128, 16, WN], BF16) for i in range(2)]
                ga = tile(st, "ga", [128, 8, WN], BF16); gb2 = tile(st, "gb2", [128, 8, WN], BF16)
                mg = tile(st, "mg", [128, 8, WN], BF16)
                hw_ = tile(st, "hw_", [128, 8, WN], F32)
                m1 = tile(st, "m1", [128, WN], F32); m2 = tile(st, "m2", [128, WN], F32)

                def chain_gen(s, w0, wn):
                    for j in range(wn // 128):
                        t0 = w0 + j * 128
                        for tl, src, nm in ((oa0, Oa[0], "Oa0"), (oa1, Oa[1], "Oa1"), (za, Za, "Za"), (ob0, Ob[0], "Ob0"), (ob1, Ob[1], "Ob1"), (gbt_, Gb, "Gb")):
                            P.dma(tl.t[:], src[s, t0:t0 + 128, :], tl, reads=[dbufs[nm]], writes=[tl.b])
                        yield
                        for (a0, a1, gz, og, nh, hd, col0, sc) in ((oa0, oa1, za, oag[j], 8, 128, 0, 1.0 / 128), (ob0, ob1, gbt_, obg[j], 4, 512, 8, 1.0 / 512)):
                            W = nh * hd
                            P.op("pool", lambda e, a0=a0, a1=a1, W=W: e.tensor_tensor(out=f1.t[:, 0:W], in0=a0.t[:], in1=a1.t[:], op=ALU.add), reads=[a0.b, a1.b], writes=[f1.b])
                            yield
                            P.op("pool", lambda e, W=W: e.tensor_tensor(out=f2.t[:, 0:W], in0=f1.t[:, 0:W], in1=f1.t[:, 0:W], op=ALU.mult), reads=[f1.b], writes=[f2.b])
                            yield
                            P.op("dve", lambda e, W=W, nh=nh, hd=hd, col0=col0: e.tensor_reduce(out=ssq.t[:, col0:col0 + nh], in_=V(f2.t, 0, 128, 0, [(hd, nh), (1, hd)]),
                                                                                             axis=AX.X, op=ALU.add), reads=[f2.b], writes=[ssq.b])
                            yield
                            rsqrt_act(ssq.t[:, col0:col0 + nh], ssq.t[:, col0:col0 + nh], sc, 1, [ssq.b], ssq.b)
                            yield
                            P.op("pool", lambda e, W=W, nh=nh, hd=hd, col0=col0: e.tensor_tensor(out=V(f1.t, 0, 128, 0, [(hd, nh), (1, hd)]), in0=V(f1.t, 0, 128, 0, [(hd, nh), (1, hd)]),
                                                                                              in1=V(ssq.t, 0, 128, col0, [(1, nh), (0, hd)]), op=ALU.mult), reads=[f1.b, ssq.b], writes=[f1.b])
                            yield
                            P.op("pool", lambda e, W=W, og=og, gz=gz: e.tensor_tensor(out=og.t[:], in0=f1.t[:, 0:W], in1=gz.t[:], op=ALU.mult), reads=[f1.b, gz.b], writes=[og.b])
                            yield

                def trans(s, w0, wn, par):
                    for j in range(wn // 128):
                        for (og, dstT, nblk) in ((oag[j], oaT[par], 8), (obg[j], obT[par], 16)):
                            for g8 in range(nblk // 8):
                                pb = bank()
                                pbv = pb.t[:, :].bitcast(BF16)
                                for c in range(8):
                                    cc = g8 * 8 + c
                                    P.op("pe", lambda e, c=c, cc=cc, og=og, pbv=pbv: e.transpose(out=pbv[:, c * 128:(c + 1) * 128], in_=og.t[:, cc * 128:(cc + 1) * 128], identity=identb.t[:, :]),
                                         reads=[og.b, identb.b], writes=[pb.b], inc=(c == 7))
                                P.op("act", lambda e, g8=g8, dstT=dstT, pbv=pbv, j=j: e.copy(out=dstT.t[:, g8 * 8:g8 * 8 + 8, j * 128:(j + 1) * 128], in_=pbv[:, 0:1024].rearrange("p (c t) -> p c t", c=8)),
                                     reads=[pb.b], accw=[dstT.b])

                def window_gen(s, w0, wn, par):
                    oaT_, obT_ = oaT[par], obT[par]
                    P.dma(ga.t[:, :, 0:wn], GaT[s, :, :, w0:w0 + wn], ga, reads=[dbufs["GaT"]], writes=[ga.b])
                    P.dma(gb2.t[:, :, 0:wn], GbT[s, :, :, w0:w0 + wn], gb2, reads=[dbufs["GbT"]], writes=[gb2.b])
                    P.dma(hw_.t[:, :, 0:wn], hT[s, :, :, w0:w0 + wn], hw_, reads=[dbufs["hT"]], writes=[hw_.b])
                    for fb in range(8):
                        pa = bank()
                        for c in range(8):
                            P.op("pe", lambda e, c=c, fb=fb, pa=pa: e.matmul(pa.t[:, 0:wn], lhsT=Wa.t[:, c, fb * 128:(fb + 1) * 128], rhs=oaT_.t[:, c, 0:wn], start=(c == 0), stop=(c == 7)),
                                 reads=[Wa.b, oaT_.b], writes=[pa.b], inc=(c == 7))
                        pbk = bank()
                        for c in range(16):
                            P.op("pe", lambda e, c=c, fb=fb, pbk=pbk: e.matmul(pbk.t[:, 0:wn], lhsT=Wb.t[:, c, fb * 128:(fb + 1) * 128], rhs=obT_.t[:, c, 0:wn], start=(c == 0), stop=(c == 15)),
                                 reads=[Wb.b, obT_.b], writes=[pbk.b], inc=(c == 15))
                        P.op("dve", lambda e, fb=fb, pa=pa: e.tensor_tensor(out=m1.t[:, 0:wn], in0=pa.t[:, 0:wn], in1=ga.t[:, fb, 0:wn], op=ALU.mult), reads=[pa.b, ga.b], writes=[m1.b])
                        P.op("dve", lambda e, fb=fb, pbk=pbk: e.tensor_tensor(out=m2.t[:, 0:wn], in0=pbk.t[:, 0:wn], in1=gb2.t[:, fb, 0:wn], op=ALU.mult), reads=[pbk.b, gb2.b], writes=[m2.b])
                        P.op("dve", lambda e, fb=fb: e.tensor_tensor(out=mg.t[:, fb, 0:wn], in0=m1.t[:, 0:wn], in1=m2.t[:, 0:wn], op=ALU.add), reads=[m1.b, m2.b],
                             writes=[mg.b] if fb == 0 else [], accw=[] if fb == 0 else [mg.b])
                        yield
                    for fb in range(8):
                        po = bank()
                        for c in range(8):
                            P.op("pe", lambda e, c=c, fb=fb, po=po: e.matmul(po.t[:, 0:wn], lhsT=Wo.t[:, c, fb * 128:(fb + 1) * 128], rhs=mg.t[:, c, 0:wn], start=(c == 0), stop=(c == 7)),
                                 reads=[Wo.b, mg.b], writes=[po.b], inc=(c == 7))
                        P.op("dve", lambda e, fb=fb, po=po: e.tensor_tensor(out=hw_.t[:, fb, 0:wn], in0=hw_.t[:, fb, 0:wn], in1=po.t[:, 0:wn], op=ALU.add), reads=[po.b, hw_.b], accw=[hw_.b])
                        yield
                    P.dma(hT[s, :, :, w0:w0 + wn], hw_.t[:, :, 0:wn], hw_, reads=[hw_.b], accw=[dbufs["hT"]])

                wins = [(s, w0, min(WN, TP - w0)) for s in range(NSEQ) for w0 in range(0, TP, WN)]
                run_rr([chain_gen(*wins[0])])
                trans(*wins[0], 0)
                for k in range(len(wins)):
                    gens = [window_gen(*wins[k], k % 2)]
                    if k + 1 < len(wins):
                        gens.append(chain_gen(*wins[k + 1]))
                    run_rr(gens)
                    if k + 1 < len(wins):
                        trans(*wins[k + 1], (k + 1) % 2)
                P.barrier([t.b for t in stg + [gn, oa0, oa1, za, ob0, ob1, gbt_, ga, gb2, hw_]])

        def stageD2(l):
            with ExitStack() as st:
                Wf1 = tile(st, "Wf1", [128, 8, 2 * FFN], BF16)
                Wf2 = tile(st, "Wf2", [128, 22, 1024], BF16)
                gs2 = load_vec_T(st, "gs2", norm_ffn[l].rearrange("(c p) -> c p", p=128), 8, 32.0)
                stw = ExitStack()
                stg = [tile(stw, "stgf%d" % i, [128, 1408], F32) for i in range(2)]
                wi = [0]

                def ldw(src, ncol, dst_ap, dstb, first, scal):
                    sg = stg[wi[0] % 2]
                    wi[0] += 1
                    P.dma(sg.t[:, 0:ncol], src, sg, writes=[sg.b])
                    if scal is not None:
                        P.op("dve", lambda e: e.tensor_scalar(out=dst_ap, in0=sg.t[:, 0:ncol], scalar1=scal, scalar2=None, op0=ALU.mult), reads=[sg.b, gs2.b],
                             writes=[dstb] if first else [], accw=[] if first else [dstb])
                    else:
                        P.op("act", lambda e: e.copy(out=dst_ap, in_=sg.t[:, 0:ncol]), reads=[sg.b], writes=[dstb] if first else [], accw=[] if first else [dstb])
                for c in range(8):
                    for pc in range(4):
                        ldw(w_ffn_in[l, c * 128:(c + 1) * 128, pc * 1408:(pc + 1) * 1408], 1408, Wf1.t[:, c, pc * 1408:(pc + 1) * 1408], Wf1.b, c == 0 and pc == 0, gs2.t[:, c:c + 1])
                for jj in range(22):
                    ldw(w_ffn_out[l, jj * 128:(jj + 1) * 128, :], 1024, Wf2.t[:, jj, :], Wf2.b, jj == 0, None)
                P.barrier([t.b for t in stg])
                stw.close()
                WN = 512
                hw_ = tile(st, "hwf", [128, 8, WN], F32); sq = tile(st, "sqf", [128, 8, WN], BF16); rr = tile(st, "rrf", [128, WN], F32)
                hn = tile(st, "hnf", [128, 8, WN], BF16); aT = tile(st, "aT", [128, 22, WN], BF16); sgt = [tile(st, "sgt%d" % i, [128, WN], F32) for i in range(2)]

                def do_window(s, w0, wn):
                    P.dma(hw_.t[:, :, 0:wn], hT[s, :, :, w0:w0 + wn], hw_, reads=[dbufs["hT"]], writes=[hw_.b])
                    P.op("act", lambda e: e.activation(out=sq.t[:, :, 0:wn], in_=hw_.t[:, :, 0:wn], func=AF.Square), reads=[hw_.b], writes=[sq.b])
                    pb = bank()
                    for c in range(8):
                        P.op("pe", lambda e, c=c: e.matmul(pb.t[:, 0:wn], lhsT=onesb.t[:, :], rhs=sq.t[:, c, 0:wn], start=(c == 0), stop=(c == 7)),
                             reads=[sq.b, onesb.b], writes=[pb.b], inc=(c == 7))
                    rsqrt_act(rr.t[:, 0:wn], pb.t[:, 0:wn], 1.0, 0, [pb.b], rr.b)
                    P.op("dve", lambda e: e.tensor_tensor(out=hn.t[:, :, 0:wn], in0=hw_.t[:, :, 0:wn], in1=V(rr.t, 0, 128, 0, [(0, 8), (1, wn)]), op=ALU.mult),
                         reads=[hw_.b, rr.b], writes=[hn.b])
                    for jj in range(22):
                        pg, pu = bank(), bank()
                        for (pp, base) in ((pg, 0), (pu, FFN)):
                            for c in range(8):
                                P.op("pe", lambda e, c=c, pp=pp, base=base, jj=jj: e.matmul(pp.t[:, 0:wn], lhsT=Wf1.t[:, c, base + jj * 128:base + (jj + 1) * 128], rhs=hn.t[:, c, 0:wn],
                                                                                            start=(c == 0), stop=(c == 7)),
                                     reads=[Wf1.b, hn.b], writes=[pp.b], inc=(c == 7))
                        sg_ = sgt[jj % 2]
                        P.op("act", lambda e, pg=pg, sg_=sg_: e.activation(out=sg_.t[:, 0:wn], in_=pg.t[:, 0:wn], func=AF.Silu), reads=[pg.b], writes=[sg_.b])
                        P.op("dve", lambda e, pu=pu, sg_=sg_, jj=jj: e.tensor_tensor(out=aT.t[:, jj, 0:wn], in0=sg_.t[:, 0:wn], in1=pu.t[:, 0:wn], op=ALU.mult), reads=[pu.b, sg_.b],
                             writes=[aT.b] if jj == 0 else [], accw=[] if jj == 0 else [aT.b])
                    for fb in range(8):
                        po = bank()
                        for jj in range(22):
                            P.op("pe", lambda e, jj=jj, fb=fb, po=po: e.matmul(po.t[:, 0:wn], lhsT=Wf2.t[:, jj, fb * 128:(fb + 1) * 128], rhs=aT.t[:, jj, 0:wn], start=(jj == 0), stop=(jj == 21)),
                                 reads=[Wf2.b, aT.b], writes=[po.b], inc=(jj == 21))
                        P.op("dve", lambda e, fb=fb, po=po: e.tensor_tensor(out=hw_.t[:, fb, 0:wn], in0=hw_.t[:, fb, 0:wn], in1=po.t[:, 0:wn], op=ALU.add), reads=[po.b, hw_.b, hn.b], accw=[hw_.b])
                    P.dma(hT[s, :, :, w0:w0 + wn], hw_.t[:, :, 0:wn], hw_, reads=[hw_.b], accw=[dbufs["hT"]])
                for s in range(NSEQ):
                    for w0 in range(0, TP, WN):
                        do_window(s, w0, min(WN, TP - w0))
                P.barrier([hw_.b])

        def stageF():
            with ExitStack() as st:
                gF = tile(st, "gF", [128, 1024], F32)
                P.dma(gF.t[:], norm_final[0:1, :].partition_broadcast(128), gF, writes=[gF.b])
                ht = [tile(st, "htF%d" % i, [128, 8, 128], F32) for i in range(2)]
                xo = [tile(st, "xoF%d" % i, [128, 1024], F32) for i in range(2)]
                x2 = tile(st, "x2F", [128, 1024], F32)
                sF = tile(st, "sF", [128, 1], F32)
                it = [0]

                def do_tile(s, n):
                    r = it[0] % 2
                    it[0] += 1
                    h_, x_ = ht[r], xo[r]
                    t0 = n * 128
                    lo = max(t0, N_META)
                    hi = min(t0 + 128, TREAL)
                    if hi <= lo:
                        return
                    P.dma(h_.t[:], hT[s, :, :, t0:t0 + 128], h_, reads=[dbufs["hT"]], writes=[h_.b])
                    for half in range(2):
                        pb = bank()
                        for cc in range(4):
                            c = half * 4 + cc
                            P.op("pe", lambda e, c=c, cc=cc, pb=pb: e.transpose(out=pb.t[:, cc * 128:(cc + 1) * 128], in_=h_.t[:, c, :], identity=identf),
                                 reads=[h_.b, cst.b], writes=[pb.b], inc=(cc == 3))
                        P.op("act", lambda e, half=half, pb=pb: e.copy(out=x_.t[:, half * 512:(half + 1) * 512], in_=pb.t[:, :]), reads=[pb.b],
                             writes=[x_.b] if half == 0 else [], accw=[] if half == 0 else [x_.b])
                    P.op("pool", lambda e: e.tensor_tensor(out=x2.t[:], in0=x_.t[:], in1=x_.t[:], op=ALU.mult), reads=[x_.b], writes=[x2.b])
                    P.op("dve", lambda e: e.tensor_reduce(out=sF.t[:, 0:1], in_=x2.t[:], axis=AX.X, op=ALU.add), reads=[x2.b], writes=[sF.b])
                    rsqrt_act(sF.t[:, 0:1], sF.t[:, 0:1], 1.0 / D, 1, [sF.b], sF.b)
                    P.op("dve", lambda e: e.tensor_scalar(out=x_.t[:], in0=x_.t[:], scalar1=sF.t[:, 0:1], scalar2=None, op0=ALU.mult), reads=[x_.b, sF.b], writes=[x_.b])
                    P.op("dve", lambda e: e.tensor_tensor(out=x_.t[:], in0=x_.t[:], in1=gF.t[:], op=ALU.mult), reads=[x_.b, gF.b], writes=[x_.b])
                    P.dma(out[s, lo - N_META:hi - N_META, :], x_.t[lo - t0:hi - t0, :], x_, reads=[x_.b], accw=[dbufs["out"]])
                for s in range(NSEQ):
                    for n in range(NT):
                        do_tile(s, n)
                P.barrier([t.b for t in ht + xo + [gF]])

        if "0" in stages:
            stage0()
        for l in range(NL):
            for s in range(NSEQ):
                if "A" in stages:
                    stageA(l, s)
            if "B" in stages:
                stageB(l)
            if "C" in stages:
                stageC(l)
            if "D" in stages:
                stageD1(l)
            if "E" in stages:
                stageD2(l)
        if "F" in stages:
            stageF()
        P.barrier()
        P.prune_incs()
        ok, stuck, val = P.check_deadlock()
        if not ok:
            raise RuntimeError("deadlock in program: %s" % ({e: (v[0], v[1], v[2][:1] + v[2][2:] if v[2] else None) for e, v in stuck.items()},))
        P.emit()
    return nc, consts_np, trig_np


_CACHE = {}


def kernel(x, meta_tokens, norm_mix, w_in, conv_w, gdn_a_log, gdn_dt_bias, gdn_norm, ret_decay_logit,
           w_up_a, w_up_b, w_out, norm_ffn, w_ffn_in, w_ffn_out, norm_final):
    x = np.asarray(x, np.float32)
    B, SEQ, _ = x.shape
    NL = np.asarray(norm_mix).shape[0]
    ncores = 8
    NSEQ = B // ncores
    key = (SEQ, NSEQ, NL)
    if key not in _CACHE:
        _CACHE[key] = build_program(SEQ + N_META, NSEQ, NL)
    nc, consts_np, trig_np = _CACHE[key]
    f = lambda a: np.ascontiguousarray(np.asarray(a, np.float32))
    common = dict(meta_tokens=f(meta_tokens), norm_mix=f(norm_mix), w_in=f(w_in), conv_w=f(conv_w),
                  gdn_a_log=f(gdn_a_log).reshape(NL, 16), gdn_dt_bias=f(gdn_dt_bias).reshape(NL, 16), gdn_norm=f(gdn_norm),
                  ret_decay_logit=f(ret_decay_logit).reshape(NL, 8), w_up_a=f(w_up_a), w_up_b=f(w_up_b), w_out=f(w_out),
                  norm_ffn=f(norm_ffn), w_ffn_in=f(w_ffn_in), w_ffn_out=f(w_ffn_out), norm_final=f(norm_final).reshape(1, D),
                  consts=consts_np, trig=trig_np)
    in_maps = []
    for c in range(ncores):
        m = dict(common)
        m["x"] = np.ascontiguousarray(x[c * NSEQ:(c + 1) * NSEQ])
        in_maps.append(m)
    res = run_bass_kernel_spmd(nc, in_maps, core_ids=list(range(ncores)))
    return np.concatenate([np.asarray(r["out"], np.float32) for r in res.results], axis=0)
```
